# Optimizing a Trainium2 kernel written in Bass

```python
import jax, jax.numpy as jnp
from jax import lax
import numpy as np

D_MODEL = 1024
BATCH = 2
SEQ = 8192
DEPTH = 4
DEC_BATCH = 128
DEC_SEQ = 1
PAST_LEN = 8192
PAGE_SIZE = 128

N_HEADS = 16
HEAD_DIM = D_MODEL // N_HEADS
N_KV_HEADS = 4
GROUP = N_HEADS // N_KV_HEADS
WINDOW = 128
BLOCK = WINDOW
CONV_WIDTH = 31
D_FF = 4 * D_MODEL
N_A_LAYERS = DEPTH // 2
N_B_LAYERS = DEPTH - N_A_LAYERS
RMS_EPS = 1e-6
LN_EPS = 1e-5

kernel_name = 'yoco_conformer_conv_swa_sink_decoder_step'


def rmsnorm(x, g):
    xf = x.astype(jnp.float32)
    y = xf * lax.rsqrt(jnp.mean(jnp.square(xf), axis=-1, keepdims=True) + RMS_EPS)
    return (y * g.astype(jnp.float32)).astype(x.dtype)


def layernorm(x, g, b):
    xf = x.astype(jnp.float32)
    mu = jnp.mean(xf, axis=-1, keepdims=True)
    var = jnp.mean(jnp.square(xf - mu), axis=-1, keepdims=True)
    y = (xf - mu) * lax.rsqrt(var + LN_EPS)
    return (y * g.astype(jnp.float32) + b.astype(jnp.float32)).astype(x.dtype)


def alibi_slopes():
    return jnp.exp2(-8.0 * jnp.arange(1, N_HEADS + 1, dtype=jnp.float32) / N_HEADS)


def causal_dwconv(u_ext, w, b):
    out = lax.conv_general_dilated(
        u_ext, w[:, None, :].astype(u_ext.dtype), window_strides=(1,), padding='VALID',
        dimension_numbers=('NWC', 'WIO', 'NWC'), feature_group_count=u_ext.shape[-1])
    return out + b


def conv_module(h, prev, w1, b1, dw, dwb, ln_g, ln_b, w2, b2):
    a = h @ w1 + b1
    u = a[..., :D_MODEL] * jax.nn.sigmoid(a[..., D_MODEL:])
    u_ext = jnp.concatenate([prev.astype(u.dtype), u], axis=1)
    c = causal_dwconv(u_ext, dw, dwb)
    c = jax.nn.silu(layernorm(c, ln_g, ln_b))
    return c @ w2 + b2, u_ext[:, -(CONV_WIDTH - 1):]


def sq_relu_mlp(h, w1, w2):
    return jnp.square(jax.nn.relu(h @ w1)) @ w2


def shared_kv(x, g, w_kv, k_norm_g):
    h = rmsnorm(x, g)
    kv = h @ w_kv
    b, t = x.shape[0], x.shape[1]
    k = kv[..., :N_KV_HEADS * HEAD_DIM].reshape(b, t, N_KV_HEADS, HEAD_DIM)
    v = kv[..., N_KV_HEADS * HEAD_DIM:].reshape(b, t, N_KV_HEADS, HEAD_DIM)
    return rmsnorm(k, k_norm_g), v


def alibi_bias(delta):
    return -alibi_slopes().reshape(N_KV_HEADS, GROUP, 1, 1) * delta.astype(jnp.float32)[None, None]


def sink_attend(q, k, v, alibi, mask, sinks):
    s = jnp.einsum('bnqkgd,bnskd->bnkgqs', q, k, preferred_element_type=jnp.float32) * (HEAD_DIM ** -0.5)
    s = jnp.where(mask[None, :, None, None], s + alibi, -jnp.inf)
    sink = sinks.astype(jnp.float32).reshape(N_KV_HEADS, GROUP, 1, 1)
    m = jnp.maximum(jnp.max(s, axis=-1, keepdims=True), sink)
    p = jnp.exp(s - m)
    denom = jnp.sum(p, axis=-1, keepdims=True) + jnp.exp(sink - m)
    return jnp.einsum('bnkgqs,bnskd->bnqkgd', p / denom, v.astype(jnp.float32))


def window_attention(h, k_ctx, v_ctx, alibi, mask, wq, q_g, sinks, wo):
    b, t = h.shape[0], h.shape[1]
    n = k_ctx.shape[1]
    q = rmsnorm((h @ wq).reshape(b, t, N_HEADS, HEAD_DIM), q_g)
    q = q.reshape(b, n, t // n, N_KV_HEADS, GROUP, HEAD_DIM)
    o = sink_attend(q, k_ctx, v_ctx, alibi, mask, sinks)
    return o.reshape(b, t, N_HEADS * HEAD_DIM).astype(h.dtype) @ wo


def setup_inputs(seed: int = 0) -> dict:
    key = jax.random.key(seed)
    ks = jax.random.split(key, 32)
    f32 = jnp.float32

    def nrm(k, shape, scale):
        return scale * jax.random.normal(k, shape, f32)

    buf = min(WINDOW, PAST_LEN)
    D = D_MODEL
    return {
        'x_prompt': nrm(ks[0], (BATCH, SEQ, D), 1.0),
        'x_sample': nrm(ks[1], (DEC_BATCH, DEC_SEQ, D), 1.0),
        'state_conv': nrm(ks[2], (N_A_LAYERS, DEC_BATCH, CONV_WIDTH - 1, D), 0.5),
        'state_win_k': nrm(ks[3], (DEC_BATCH, buf, N_KV_HEADS, HEAD_DIM), 1.0),
        'state_win_v': nrm(ks[4], (DEC_BATCH, buf, N_KV_HEADS, HEAD_DIM), 1.0),
        'norm_mix_g': 1.0 + nrm(ks[5], (DEPTH, D), 0.02),
        'norm_mlp_g': 1.0 + nrm(ks[6], (DEPTH, D), 0.02),
        'conv_w1': nrm(ks[7], (N_A_LAYERS, D, 2 * D), D ** -0.5),
        'conv_b1': nrm(ks[8], (N_A_LAYERS, 2 * D), 0.02),
        'conv_dw': nrm(ks[9], (N_A_LAYERS, CONV_WIDTH, D), CONV_WIDTH ** -0.5),
        'conv_dwb': nrm(ks[10], (N_A_LAYERS, D), 0.02),
        'conv_ln_g': 1.0 + nrm(ks[11], (N_A_LAYERS, D), 0.02),
        'conv_ln_b': nrm(ks[12], (N_A_LAYERS, D), 0.02),
        'conv_w2': nrm(ks[13], (N_A_LAYERS, D, D), D ** -0.5),
        'conv_b2': nrm(ks[14], (N_A_LAYERS, D), 0.02),
        'kv_norm_g': 1.0 + nrm(ks[15], (D,), 0.02),
        'w_kv': nrm(ks[16], (D, 2 * N_KV_HEADS * HEAD_DIM), D ** -0.5),
        'k_norm_g': 1.0 + nrm(ks[17], (HEAD_DIM,), 0.02),
        'attn_wq': nrm(ks[18], (N_B_LAYERS, D, N_HEADS * HEAD_DIM), D ** -0.5),
        'q_norm_g': 1.0 + nrm(ks[19], (N_B_LAYERS, HEAD_DIM), 0.02),
        'attn_sinks': nrm(ks[20], (N_B_LAYERS, N_HEADS), 0.5),
        'attn_wo': nrm(ks[21], (N_B_LAYERS, N_HEADS * HEAD_DIM, D), (N_HEADS * HEAD_DIM) ** -0.5),
        'mlp_w1': nrm(ks[22], (DEPTH, D, D_FF), D ** -0.5),
        'mlp_w2': nrm(ks[23], (DEPTH, D_FF, D), D_FF ** -0.5),
    }


def reference(x_prompt, x_sample, state_conv, state_win_k, state_win_v,
              norm_mix_g, norm_mlp_g,
              conv_w1, conv_b1, conv_dw, conv_dwb, conv_ln_g, conv_ln_b, conv_w2, conv_b2,
              kv_norm_g, w_kv, k_norm_g,
              attn_wq, q_norm_g, attn_sinks, attn_wo,
              mlp_w1, mlp_w2):
    xp, xs = x_prompt, x_sample
    bp, sp = xp.shape[0], xp.shape[1]
    bs, ts = xs.shape[0], xs.shape[1]
    n_blk = sp // BLOCK

    qi = jnp.arange(BLOCK)[:, None]
    kj = jnp.arange(2 * BLOCK)[None, :]
    delta_p = BLOCK + qi - kj
    band_p = (delta_p >= 0) & (delta_p <= WINDOW)
    mask_p = band_p[None] & ((jnp.arange(n_blk)[:, None, None] > 0) | (kj[None] >= BLOCK))
    alibi_p = alibi_bias(delta_p)

    buf_len = state_win_k.shape[1]
    si = jnp.arange(ts)[:, None]
    sj = jnp.arange(buf_len + ts)[None, :]
    delta_s = buf_len + si - sj
    mask_s = ((delta_s >= 0) & (delta_s <= WINDOW))[None]
    alibi_s = alibi_bias(delta_s)

    conv_prev_p = jnp.zeros((bp, CONV_WIDTH - 1, D_MODEL), xp.dtype)
    conv_new_p, conv_new_s = [], []
    kp_ctx = vp_ctx = ks_ctx = vs_ctx = None
    new_kp = new_vp = new_ks = new_vs = None

    for l in range(DEPTH):
        if l < N_A_LAYERS:
            a = l
            cw = (conv_w1[a], conv_b1[a], conv_dw[a], conv_dwb[a], conv_ln_g[a], conv_ln_b[a], conv_w2[a], conv_b2[a])
            yp, cp = conv_module(rmsnorm(xp, norm_mix_g[l]), conv_prev_p, *cw)
            ys, cs = conv_module(rmsnorm(xs, norm_mix_g[l]), state_conv[a], *cw)
            xp, xs = xp + yp, xs + ys
            conv_new_p.append(cp)
            conv_new_s.append(cs)
        else:
            if l == N_A_LAYERS:
                kp, vp = shared_kv(xp, kv_norm_g, w_kv, k_norm_g)
                kb = kp.reshape(bp, n_blk, BLOCK, N_KV_HEADS, HEAD_DIM)
                vb = vp.reshape(bp, n_blk, BLOCK, N_KV_HEADS, HEAD_DIM)
                shift = ((0, 0), (1, 0), (0, 0), (0, 0), (0, 0))
                kp_ctx = jnp.concatenate([jnp.pad(kb[:, :-1], shift), kb], axis=2)
                vp_ctx = jnp.concatenate([jnp.pad(vb[:, :-1], shift), vb], axis=2)
                keep_p = min(WINDOW, sp)
                new_kp, new_vp = kp[:, -keep_p:], vp[:, -keep_p:]

                kn, vn = shared_kv(xs, kv_norm_g, w_kv, k_norm_g)
                k_all = jnp.concatenate([state_win_k.astype(kn.dtype), kn], axis=1)
                v_all = jnp.concatenate([state_win_v.astype(vn.dtype), vn], axis=1)
                ks_ctx, vs_ctx = k_all[:, None], v_all[:, None]
                new_ks, new_vs = k_all[:, -buf_len:], v_all[:, -buf_len:]
            bl = l - N_A_LAYERS
            aw = (attn_wq[bl], q_norm_g[bl], attn_sinks[bl], attn_wo[bl])
            xp = xp + window_attention(rmsnorm(xp, norm_mix_g[l]), kp_ctx, vp_ctx, alibi_p, mask_p, *aw)
            xs = xs + window_attention(rmsnorm(xs, norm_mix_g[l]), ks_ctx, vs_ctx, alibi_s, mask_s, *aw)
        xp = xp + sq_relu_mlp(rmsnorm(xp, norm_mlp_g[l]), mlp_w1[l], mlp_w2[l])
        xs = xs + sq_relu_mlp(rmsnorm(xs, norm_mlp_g[l]), mlp_w1[l], mlp_w2[l])

    new_conv_p = jnp.stack(conv_new_p, axis=0)
    new_conv_s = jnp.stack(conv_new_s, axis=0)
    return (xp, xs, new_conv_p, new_conv_s, new_kp, new_vp, new_ks, new_vs)
```

```python
from contextlib import ExitStack
import numpy as np
import concourse.bass as bass
import concourse.mybir as mybir
from concourse.bass_utils import run_bass_kernel_spmd

F32 = mybir.dt.float32
BF16 = mybir.dt.bfloat16
ALU = mybir.AluOpType
AF = mybir.ActivationFunctionType
AX = mybir.AxisListType

D = 1024
NCH = 8
H = 192
NO = 2048
NS = 16
TP = H + NO
T = TP + NS
NSLOT = 4
SLOT = 4096
NKB = 17
KVW = NKB * 128
N_HEADS = 16
COMPUTE = ("pe", "act", "dve", "pool")
NDMASEM = 8

VC = {}
_c = 0
for _n, _w in (("nmix", 32), ("nmlp", 32), ("kvn", 8), ("b1", 32), ("dwb", 16), ("lng", 16),
               ("lnb", 16), ("b2", 16), ("dwT", 2 * 31 * 8), ("kng", 1), ("qng", 2), ("sink", 32)):
    VC[_n] = _c
    _c += _w
NV = _c


def perm_head(p):
    c, hf = p // 2, p % 2
    if c < 4:
        return c if hf == 0 else 4 + c
    return 8 + (c - 4) if hf == 0 else 12 + (c - 4)


PERM = [perm_head(p) for p in range(16)]
SLOPES = [2.0 ** (-8.0 * (h + 1) / 16.0) for h in range(16)]


class Op:
    __slots__ = ("eng", "fn", "deps", "idx", "signal", "cnt", "dma", "dslot", "dcnt", "prevdma")

    def __init__(self, eng, fn, dma):
        self.eng = eng
        self.fn = fn
        self.deps = []
        self.signal = False
        self.cnt = 0
        self.dma = dma
        self.prevdma = None


class Prog:
    def __init__(self, nc, stack):
        self.nc = nc
        self.eobj = {"pe": nc.tensor, "act": nc.scalar, "dve": nc.vector,
                     "pool": nc.gpsimd, "sp": nc.sync}
        self.ops = {e: [] for e in self.eobj}
        self.order = []
        self.lastw = {}
        self.readers = {}
        self.sem = {e: stack.enter_context(nc.semaphore("s_" + e)) for e in COMPUTE}
        self.dsem = {e: [stack.enter_context(nc.semaphore("d_%s%d" % (e, i))) for i in range(NDMASEM)]
                     for e in ("sp", "pool")}
        self.dmaops = {e: [] for e in self.eobj}
        self.pending = {}

    def fence(self):
        fs = set()
        for e, lst in self.ops.items():
            comp = [o for o in lst if not o.dma]
            if comp:
                fs.add(comp[-1])
            if e != "pool":
                for o in self.dmaops[e][-NDMASEM:]:
                    fs.add(o)
        for e in self.eobj:
            self.pending[e] = set(fs) | self.pending.get(e, set())

    def op(self, eng, fn, reads=(), writes=(), dma=False, nofence=False):
        o = Op(eng, fn, dma)
        deps = set()
        if eng in self.pending and not nofence:
            deps |= self.pending.pop(eng)
        for r in reads:
            w = self.lastw.get(r)
            if w is not None:
                deps.add(w)
        for r in writes:
            w = self.lastw.get(r)
            if w is not None:
                deps.add(w)
            rl = self.readers.get(r)
            if rl:
                for kk, vv in rl.items():
                    if kk == "dma":
                        deps.update(vv)
                    else:
                        deps.add(vv)
        o.deps = list(deps)
        for r in reads:
            rd = self.readers.setdefault(r, {})
            if dma:
                rd.setdefault("dma", []).append(o)
            else:
                rd[eng] = o
        for r in writes:
            self.lastw[r] = o
            self.readers[r] = {}
        o.idx = len(self.ops[eng])
        self.ops[eng].append(o)
        self.order.append(o)
        if dma:
            q = self.dmaops[eng]
            o.dslot = len(q) % NDMASEM
            o.dcnt = 16 * (len(q) // NDMASEM + 1)
            o.prevdma = q[-NDMASEM] if len(q) >= NDMASEM else None
            q.append(o)
        return o

    def pe(self, fn, reads=(), writes=()):
        return self.op("pe", fn, reads, writes)

    def act(self, fn, reads=(), writes=()):
        return self.op("act", fn, reads, writes)

    def dve(self, fn, reads=(), writes=()):
        return self.op("dve", fn, reads, writes)

    def dma(self, eng, out, in_, reads=(), writes=(), nofence=False):
        return self.op(eng, lambda e: e.dma_start(out=out, in_=in_), reads, writes, dma=True, nofence=nofence)

    def emit(self):
        for o in self.order:
            for d in o.deps:
                if d.dma:
                    continue
                if d.eng == "pe" and o.eng == "pe" and not o.dma:
                    continue
                d.signal = True
        for e in COMPUTE:
            c = 0
            for o in self.ops[e]:
                if o.signal and not o.dma:
                    c += 1
                    o.cnt = c
        for e, lst in self.ops.items():
            eng = self.eobj[e]
            waited = {}
            for o in lst:
                need = {}
                for d in o.deps:
                    if d.dma:
                        key = ("d", d.eng, d.dslot)
                        val = d.dcnt
                    else:
                        if d.eng == "pe" and e == "pe" and not o.dma:
                            continue
                        key = ("c", d.eng)
                        val = d.cnt
                    if need.get(key, 0) < val:
                        need[key] = val
                if o.dma and o.prevdma is not None:
                    key = ("d", e, o.dslot)
                    need[key] = max(need.get(key, 0), o.prevdma.dcnt)
                for key, val in need.items():
                    if waited.get(key, 0) >= val:
                        continue
                    waited[key] = val
                    sem = self.sem[key[1]] if key[0] == "c" else self.dsem[key[1]][key[2]]
                    eng.wait_ge(sem, val)
                ins = o.fn(eng)
                if o.dma:
                    ins.then_inc(self.dsem[e][o.dslot], 16)
                elif o.signal:
                    ins.then_inc(self.sem[e], 1)
        for e, q in self.dmaops.items():
            eng = self.eobj[e]
            for o in q[-NDMASEM:]:
                eng.wait_ge(self.dsem[e][o.dslot], o.dcnt)


class DryProg:
    def fence(self):
        pass

    def op(self, *a, **k):
        return None

    pe = act = dve = dma = op


def V(t, p0, npart, off, dims):
    F = 1
    for s in t.shape[1:]:
        F *= s
    return bass.AP(t, p0 * F + off, [[F, npart]] + [list(d) for d in dims])


def R(name, c, a, n):
    return [(name, c, b) for b in range(a // 32, (a + n - 1) // 32 + 1)]


def groups(lo, hi, step=512):
    tot = hi - lo
    k = -(-tot // step)
    w = -(-tot // k)
    w = -(-w // 8) * 8
    out = []
    a = lo
    while a < hi:
        n = min(w, hi - a)
        out.append((a, n))
        a += n
    assert len(out) == k and all(n <= step for _, n in out), out
    return out


def pid_w1(l, i):
    return 22 * l + i


def pid_w2(l, i):
    return 22 * l + 4 + i


def pid_mlp(l, e, which):
    base = 22 * l + 6 if l < 2 else 45 + 20 * (l - 2) + 4
    return base + 2 * e + which


PID_KV = 44


def pid_wq(l, i):
    return 45 + 20 * (l - 2) + i


def pid_wo(l, i):
    return 45 + 20 * (l - 2) + 2 + i


NPIECE = 85


class Ring:
    def __init__(self, P, Wt, wst, sched):
        self.P, self.Wt, self.wst = P, Wt, wst
        self.sched = sched
        self.dry = sched is None
        self.log = []
        self.nacq = 0
        self.nload = 0
        self.released = set()

    def acquire(self, pid):
        seq = self.nacq
        self.nacq += 1
        if self.dry:
            self.log.append(pid)
            return seq, seq % NSLOT
        assert self.sched[seq] == pid, (seq, pid, self.sched[seq])
        self._pump()
        assert self.nload > seq, "piece %d not loadable (slot not released)" % seq
        return seq, seq % NSLOT

    def release(self, seq):
        if self.dry:
            return
        self.released.add(seq)
        self._pump()

    def _pump(self):
        while self.nload < len(self.sched) and (self.nload < NSLOT or (self.nload - NSLOT) in self.released):
            s = self.nload % NSLOT
            pid = self.sched[self.nload]
            self.P.dma("pool", V(self.Wt, 0, 128, s * SLOT, [[1, SLOT]]), self.wst[pid], writes=[("w", s)], nofence=True)
            self.nload += 1


def build_program(debug=False):
    nc = bass.Bass("TRN2", target_bir_lowering=False)
    dt = nc.dram_tensor
    xin = dt("xin", [TP, D], F32, kind="ExternalInput")
    xs = dt("xs", [NS, D], F32, kind="ExternalInput")
    sconv = dt("sconv", [2, NS, 30, D], F32, kind="ExternalInput")
    swk = dt("swk", [NS, 128, 256], F32, kind="ExternalInput")
    swv = dt("swv", [NS, 128, 256], F32, kind="ExternalInput")
    hm_d = dt("hm", [128, 2], F32, kind="ExternalInput")
    vecs_d = dt("vecs", [128, NV], F32, kind="ExternalInput")
    gtm_d = dt("gtm", [NS, 3 * 64], F32, kind="ExternalInput")
    cdw = dt("cdw", [2, 31, D], F32, kind="ExternalInput")
    wst = dt("wst", [NPIECE, 128, SLOT], F32, kind="ExternalInput")
    ident_d = dt("ident", [128, 128], F32, kind="ExternalInput")
    bd64_d = dt("bd64", [128, 128], F32, kind="ExternalInput")
    sel4_d = dt("sel4", [120, 4], F32, kind="ExternalInput")
    i16_d = dt("i16", [16, 16], F32, kind="ExternalInput")
    emask_d = dt("emask", [128, 4096], F32, kind="ExternalInput")
    swp_d = dt("swp", [128, 128], F32, kind="ExternalInput")
    ebs_d = dt("ebs", [128, 16], F32, kind="ExternalInput")

    y_o = dt("y", [NO + NS, D], F32, kind="ExternalOutput")
    ncp_o = dt("ncp", [2, 30, D], F32, kind="ExternalOutput")
    ncs_o = dt("ncs", [2, NS, 30, D], F32, kind="ExternalOutput")
    nkp_o = dt("nkp", [128, 256], F32, kind="ExternalOutput")
    nvp_o = dt("nvp", [128, 256], F32, kind="ExternalOutput")
    nks_o = dt("nks", [NS, 128, 256], F32, kind="ExternalOutput")
    nvs_o = dt("nvs", [NS, 128, 256], F32, kind="ExternalOutput")

    with ExitStack() as st:
        def sb(name, shape, dtp=F32):
            return st.enter_context(nc.sbuf_tensor(name, shape, dtp))

        xT = sb("xT", [128, NCH * T])
        AT = sb("AT", [128, NCH * T], BF16)
        Wt = sb("Wt", [128, NSLOT * SLOT], BF16)
        hid = sb("hid", [128, 5 * 512], BF16)
        ident32 = sb("ident32", [128, 128])
        ones32 = sb("ones32", [128, 128])
        bd64 = sb("bd64s", [128, 128])
        identb = sb("identb", [128, 128], BF16)
        onesb = sb("onesb", [128, 128], BF16)
        vecs = sb("vecss", [128, NV])
        hm = sb("hms", [128, 2])
        esink = sb("esink", [128, 32])
        qng8 = sb("qng8", [128, 2])
        i16 = sb("i16s", [16, 16])
        sel4 = sb("sel4s", [120, 4])
        gtm = sb("gtms", [16, 192])
        ebs = sb("ebss", [128, 16])
        u32p = sb("u32p", [128, 8 * 30])
        u32s = sb("u32s", [128, 8 * 16])
        K32 = sb("K32", [128, 256])
        kvs = sb("kvs", [16, 512])
        knew = sb("knew", [16, 256])
        VnT = sb("VnT", [128, 32])
        bd64b = sb("bd64b", [128, 128], BF16)
        swp = sb("swps", [128, 128])
        dnt = sb("dnt", [128, 8])
        scs = sb("scs", [128, 128])
        rq = sb("rq", [16, 16])
        sn = sb("sn", [16, 16])
        rk = sb("rk", [16, 4])
        ssb = sb("ssb", [128, 16])
        pbm = sb("pbm", [128, 16])
        NU = 24448 + 1024
        U = sb("U", [128, NU], BF16)
        pb = [st.enter_context(nc.psum_tensor("pb%d" % i, [128, 512], F32)) for i in range(8)]

        class Carve:
            def __init__(self, t, lo, hi):
                self.t, self.cur, self.hi = t, lo, hi

            def f32(self, nel):
                off = self.cur
                self.cur += 2 * nel
                assert self.cur <= self.hi, (self.cur, self.hi)
                return off

            def b16(self, nel):
                off = self.cur
                self.cur += nel
                assert self.cur <= self.hi, (self.cur, self.hi)
                return off

        cu = Carve(U, 0, NU)
        o_tmp = [cu.f32(512) for _ in range(4)]
        o_g8 = cu.b16(8 * 512)
        ubase = cu.cur
        ca = Carve(U, ubase, NU)
        o_call = ca.f32(8 * 512)
        o_dg = [ca.b16(31 * 128) for _ in range(2)]
        o_dwrep = o_g8
        cb = Carve(U, ubase, NU)
        o_E = cb.b16(4096)
        o_KT = cb.b16(4 * KVW)
        o_Vtm = cb.b16(NKB * 260)
        cat = Carve(AT, 0, NCH * T)
        o_qg = cat.b16(8 * 512)
        o_og = cat.b16(8 * 512)
        abase = cat.cur
        o_P0 = [cat.b16(1024) for _ in range(2)]
        o_PT = [cat.b16(1024) for _ in range(3)]
        o_ogt = [cat.b16(1024) for _ in range(2)]
        cs_ = Carve(AT, abase, NCH * T)
        o_prod = cs_.f32(1024)
        o_KsT = [cs_.b16(256) for _ in range(2)]
        o_QsT = cs_.b16(128)
        o_ssa = cs_.f32(256)
        o_pba = cs_.f32(256)
        o_qtm = cs_.f32(1024)
        o_kst = [cs_.f32(256) for _ in range(2)]
        o_vst = [cs_.f32(256) for _ in range(2)]
        o_dD = cs_.f32(256)
        o_pnb = cs_.f32(256)
        o_on = cs_.f32(256)
        o_dt = cs_.f32(256)

        def UF(off, p0, npart, dims, extra=0):
            d2 = [[2 * s, c] for s, c in dims[:-1]] + [[1, 2 * dims[-1][1]]]
            assert dims[-1][0] == 1
            return V(U, p0, npart, off + 2 * extra, d2).bitcast(F32)

        def AF32(off, p0, npart, dims, extra=0):
            d2 = [[2 * s, c] for s, c in dims[:-1]] + [[1, 2 * dims[-1][1]]]
            assert dims[-1][0] == 1
            return V(AT, p0, npart, off + 2 * extra, d2).bitcast(F32)

        def tmp(i, n, p0=0, npart=128):
            return UF(o_tmp[i], p0, npart, [[1, n]])

        def tmpb(i, n):
            return V(U, 0, 128, o_tmp[i], [[1, n]])

        def xTv(c, a, n):
            return V(xT, 0, 128, c * T + a, [[1, n]])

        def ATv(c, a, n):
            return V(AT, 0, 128, c * T + a, [[1, n]])

        def g8(c, n, a=0):
            return V(U, 0, 128, o_g8 + c * 512 + a, [[1, n]])

        def call(c, a, n):
            return UF(o_call, 0, 128, [[1, n]], extra=c * 512 + a)

        def Wv(s, off, n):
            return V(Wt, 0, 128, s * SLOT + off, [[1, n]])

        def vcol(name, i):
            return V(vecs, 0, 128, VC[name] + i, [[1, 1]])

        def bank(i, n, a=0, p0=0, npart=128):
            return V(pb[i], p0, npart, a, [[1, n]])

        def emit_all(P, W):
            cnt = {"stat": 0, "hb": 0, "yb": 0, "ok": 0}

            def MM(out, lhsT, rhs, start, stop, reads, writes, skip=False):
                P.op("pe", lambda e: e.matmul(out, lhsT=lhsT, rhs=rhs, start=start, stop=stop, skip_group_check=skip),
                     reads, writes)

            def TR(out, in_, idn, reads, writes):
                P.op("pe", lambda e: e.transpose(out, in_, idn), reads, writes)

            def ACTV(out, in_, func, reads, writes, bias=None, scale=1.0):
                if bias is None:
                    P.op("act", lambda e: e.activation(out=out, in_=in_, func=func, scale=scale), reads, writes)
                else:
                    P.op("act", lambda e: e.activation(out=out, in_=in_, func=func, bias=bias, scale=scale), reads, writes)

            def CP(eng, out, in_, reads, writes):
                if eng == "act":
                    P.op("act", lambda e: e.copy(out=out, in_=in_), reads, writes)
                else:
                    P.op("dve", lambda e: e.tensor_copy(out=out, in_=in_), reads, writes)

            def TT(out, in0, in1, op, reads, writes):
                P.op("dve", lambda e: e.tensor_tensor(out=out, in0=in0, in1=in1, op=op), reads, writes)

            def STT(out, in0, scalar, in1, op0, op1, reads, writes):
                P.op("dve", lambda e: e.scalar_tensor_tensor(out=out, in0=in0, scalar=scalar, in1=in1, op0=op0, op1=op1),
                     reads, writes)

            def TS(out, in0, scalar1, op0, reads, writes):
                P.op("dve", lambda e: e.tensor_scalar(out=out, in0=in0, scalar1=scalar1, scalar2=None, op0=op0), reads, writes)

            def RCP(out, in_, reads, writes):
                P.op("dve", lambda e: e.reciprocal(out=out, in_=in_), reads, writes)

            def RED(out, in_, reads, writes):
                P.op("dve", lambda e: e.tensor_reduce(out=out, in_=in_, axis=AX.X, op=ALU.add), reads, writes)

            def BK(i):
                return ("bank", i)

            def RSTD(out, in_, scale, eps, reads, writes):
                ACTV(out, in_, AF.Ln, reads, writes, bias=eps, scale=scale)
                ACTV(out, out, AF.Exp, [], writes, scale=-0.5)

            def POOL_TT(out, in0, in1, op, reads, writes):
                P.op("pool", lambda e: e.tensor_tensor(out=out, in0=in0, in1=in1, op=op), reads, writes)

            def POOL_TS(out, in0, scalar1, op0, reads, writes):
                P.op("pool", lambda e: e.tensor_scalar(out=out, in0=in0, scalar1=scalar1, scalar2=None, op0=op0), reads, writes)

            P.dma("sp", ident32[:], ident_d[:], writes=["ident32"])
            P.dma("sp", bd64[:], bd64_d[:], writes=["bd64"])
            P.dma("sp", vecs[:], vecs_d[:], writes=["vecs"])
            P.dma("sp", hm[:], hm_d[:], writes=["hm"])
            P.dma("sp", i16[:], i16_d[:], writes=["i16"])
            P.dma("sp", sel4[:], sel4_d[:], writes=["sel4"])
            P.dma("sp", gtm[:], gtm_d[:], writes=["gtm"])
            P.dma("sp", ebs[:], ebs_d[:], writes=["ebs"])
            P.dma("sp", swp[:], swp_d[:], writes=["swp"])
            P.op("dve", lambda e: e.memset(ones32[:], 1.0), (), ["ones32"])
            P.op("dve", lambda e: e.memset(onesb[:], 1.0), (), ["onesb"])
            CP("dve", identb[:], ident32[:], ["ident32"], ["identb"])
            CP("dve", bd64b[:], bd64[:], ["bd64"], ["bd64b"])
            ACTV(esink[:], V(vecs, 0, 128, VC["sink"], [[1, 32]]), AF.Exp, ["vecs"], ["esink"])
            TS(qng8[:], V(vecs, 0, 128, VC["qng"], [[1, 2]]), 0.125, ALU.mult, ["vecs"], ["qng8"])

            def load_tok(src_ap, nr, a, k):
                stg = UF(o_call, 0, nr, [[1, 1024]], extra=(k % 4) * 1024)
                P.dma("sp", stg, src_ap, writes=[("io", k % 4)])
                for half in range(2):
                    bi = (2 * k + half) % 4
                    for cc in range(4):
                        c = 4 * half + cc
                        TR(V(pb[bi], 0, 128, cc * 128, [[1, nr]]),
                           UF(o_call, 0, nr, [[1, 128]], extra=(k % 4) * 1024 + c * 128),
                           V(ident32, 0, nr, 0, [[1, nr]]), [("io", k % 4), "ident32"], [BK(bi)])
                    wr = []
                    for cc in range(4):
                        wr += R("x", 4 * half + cc, a, nr)
                    CP("act" if (2 * k + half) % 2 == 0 else "dve",
                       V(xT, 0, 128, 4 * half * T + a, [[T, 4], [1, nr]]), V(pb[bi], 0, 128, 0, [[128, 4], [1, nr]]),
                       [], wr + [BK(bi)])

            xl = {"k": 0, "r0": 0, "done": False}

            def ensure_x(upto):
                while xl["r0"] < min(upto, TP):
                    nr = min(128, TP - xl["r0"])
                    load_tok(xin[xl["r0"]:xl["r0"] + nr, :], nr, xl["r0"], xl["k"])
                    xl["k"] += 1
                    xl["r0"] += nr
                if upto > TP and not xl["done"]:
                    load_tok(xs[:, :], NS, TP, xl["k"])
                    xl["k"] += 1
                    xl["done"] = True
                    for a_ in range(2):
                        P.dma("sp", ncs_o[a_, :, 0:29, :], sconv[a_, :, 1:30, :])
                    P.dma("sp", nks_o[:, 0:127, :], swk[:, 1:128, :])
                    P.dma("sp", nvs_o[:, 0:127, :], swv[:, 1:128, :])

            def norm_stats(a, n, slot):
                bi = 6 + cnt["stat"] % 2
                cnt["stat"] += 1
                for c in range(NCH):
                    ACTV(tmpb(c % 2, n), xTv(c, a, n), AF.Square, R("x", c, a, n), [("tmp", c % 2)])
                    MM(bank(bi, n), onesb[:], tmpb(c % 2, n), c == 0, c == NCH - 1, [("tmp", c % 2), "onesb"], [BK(bi)])
                RSTD(tmp(2 + slot, n), bank(bi, n), 1.0 / D, 1e-6, [], [("tmp", 2 + slot), BK(bi)])

            def norm_apply(gname, gi0, a, n, dst, dres, slot):
                for c in range(NCH):
                    STT(dst(c), xTv(c, a, n), vcol(gname, gi0 + c), tmp(2 + slot, n), ALU.mult, ALU.mult,
                        R("x", c, a, n) + [("tmp", 2 + slot), "vecs"], dres(c))

            def norm_group(gname, gi0, a, n, dst, dres):
                slot = cnt["stat"] % 2
                norm_stats(a, n, slot)
                norm_apply(gname, gi0, a, n, dst, dres, slot)

            def out_tok(dst_ap, nr, a, k):
                for half in range(2):
                    bi = 6 + half
                    for cc in range(4):
                        c = 4 * half + cc
                        TR(V(pb[bi], 0, nr, cc * 128, [[1, 128]]), xTv(c, a, nr), ident32[:],
                           R("x", c, a, nr) + ["ident32"], [BK(bi)])
                    CP("act" if half == 0 else "dve",
                       UF(o_g8, 0, nr, [[1, 512]], extra=(k % 2) * 1024 + half * 512), V(pb[bi], 0, nr, 0, [[1, 512]]),
                       [], [("oio", k % 2, half), BK(bi)])
                P.dma("sp", dst_ap, UF(o_g8, 0, nr, [[1, 1024]], extra=(k % 2) * 1024),
                      reads=[("oio", k % 2, 0), ("oio", k % 2, 1)])

            def emit_out(a, n):
                c = a
                while c < a + n:
                    if c >= TP:
                        out_tok(y_o[NO:NO + NS, :], NS, TP, cnt["ok"])
                        cnt["ok"] += 1
                        c += NS
                    else:
                        nr = min(128, min(a + n, TP) - c)
                        r = c - H
                        out_tok(y_o[r:r + nr, :], nr, c, cnt["ok"])
                        cnt["ok"] += 1
                        c += nr

            def mlp_norm(l, a, n):
                norm_group("nmlp", 8 * l, a, n, lambda c, a=a, n=n: ATv(c, a, n), lambda c, a=a, n=n: R("at", c, a, n))

            def mlp(l, grps, do_norm=True, after_unit=None):
                nrm = {"n": 0 if do_norm else len(grps)}

                def need_norm(g):
                    while nrm["n"] <= min(g, len(grps) - 1):
                        mlp_norm(l, grps[nrm["n"]][0], grps[nrm["n"]][1])
                        nrm["n"] += 1

                need_norm(1)
                units = [(e8, gi) for e8 in range(8) for gi in range(len(grps))]
                acq = {}

                def ensure(e8):
                    if e8 not in acq:
                        q1, s1 = W.acquire(pid_mlp(l, e8, 0))
                        q2, s2 = W.acquire(pid_mlp(l, e8, 1))
                        acq[e8] = (q1, s1, q2, s2)

                def hslot(ui, jb):
                    return (4 * ui + jb) % 5

                def w1_block(ui, jb):
                    e8, gi = units[ui]
                    ensure(e8)
                    a, n = grps[gi]
                    s1 = acq[e8][1]
                    hs = hslot(ui, jb)
                    bi = cnt["hb"] % 3
                    cnt["hb"] += 1
                    for kc in range(NCH):
                        MM(bank(bi, n), Wv(s1, kc * 512 + jb * 128, 128), ATv(kc, a, n), kc == 0, kc == NCH - 1,
                           [("w", s1)] + R("at", kc, a, n), [BK(bi)])
                    ti = cnt["hb"] % 2
                    ACTV(tmp(ti, n), bank(bi, n), AF.Relu, [], [("tmp", ti), BK(bi)])
                    ACTV(V(hid, 0, 128, hs * 512, [[1, n]]), tmp(ti, n), AF.Square, [("tmp", ti)], [("hid", hs)])

                w1_block(0, 0)
                for ui in range(len(units)):
                    e8, gi = units[ui]
                    a, n = grps[gi]
                    need_norm(gi + 2)
                    for jb in range(1, 4):
                        w1_block(ui, jb)
                    if ui + 1 < len(units):
                        w1_block(ui + 1, 0)
                    if gi == len(grps) - 1:
                        W.release(acq[e8][0])
                    s2 = acq[e8][3]
                    for oc in range(NCH):
                        bi = 3 + cnt["yb"] % 3
                        cnt["yb"] += 1
                        for jb in range(4):
                            hs = hslot(ui, jb)
                            MM(bank(bi, n), Wv(s2, jb * 1024 + oc * 128, 128), V(hid, 0, 128, hs * 512, [[1, n]]),
                               jb == 0, jb == 3, [("w", s2), ("hid", hs)], [BK(bi)])
                        TT(xTv(oc, a, n), bank(bi, n), xTv(oc, a, n), ALU.add, [], R("x", oc, a, n) + [BK(bi)])
                    if l == 3 and e8 == 7:
                        emit_out(a, n)
                    if after_unit is not None:
                        after_unit(e8, gi)
                    if gi == len(grps) - 1:
                        W.release(acq[e8][2])

            sgm = V(hid, 0, 128, 0, [[1, 2048]]).bitcast(F32)

            def sgv(a0, n):
                return sgm[:, a0:a0 + n]

            for l in range(2):
                lo_i = 0 if l == 0 else 32
                lo_ii = 32 if l == 0 else 64
                g_i = groups(lo_i, T)
                ensure_x(g_i[0][0] + g_i[0][1])
                norm_stats(g_i[0][0], g_i[0][1], 0)
                for gidx, (a, n) in enumerate(g_i):
                    norm_apply("nmix", 8 * l, a, n, lambda c, n=n: g8(c, n), lambda c: [("g8", c)], gidx % 2)
                    if gidx + 1 < len(g_i):
                        ensure_x(g_i[gidx + 1][0] + g_i[gidx + 1][1])
                        norm_stats(g_i[gidx + 1][0], g_i[gidx + 1][1], (gidx + 1) % 2)
                    for i in range(4):
                        q, s = W.acquire(pid_w1(l, i))
                        for cc in range(2):
                            uc = 2 * i + cc
                            bA, bB = (0, 1) if uc % 2 == 0 else (2, 3)
                            for kc in range(NCH):
                                MM(bank(bA, n), Wv(s, kc * 512 + cc * 128, 128), g8(kc, n), kc == 0, kc == NCH - 1,
                                   [("w", s), ("g8", kc)], [BK(bA)])
                            for kc in range(NCH):
                                MM(bank(bB, n), Wv(s, kc * 512 + 256 + cc * 128, 128), g8(kc, n), kc == 0, kc == NCH - 1,
                                   [("w", s), ("g8", kc)], [BK(bB)])
                            ACTV(sgv(0, n), bank(bB, n), AF.Sigmoid, ["vecs"], ["sgm", BK(bB)],
                                 bias=vcol("b1", 16 * l + 8 + uc))
                            STT(ATv(uc, a, n), bank(bA, n), vcol("b1", 16 * l + uc), sgv(0, n), ALU.add, ALU.mult,
                                ["sgm", "vecs"], R("at", uc, a, n) + [BK(bA)])
                            if a <= TP - 30 and a + n >= T:
                                o1 = TP - 30 - a
                                STT(V(u32p, 0, 128, uc * 30, [[1, 30]]), bank(bA, 30, o1), vcol("b1", 16 * l + uc),
                                    sgv(o1, 30), ALU.add, ALU.mult, ["sgm", "vecs"], [("u32p", uc), BK(bA)])
                                o2 = TP - a
                                STT(V(u32s, 0, 128, uc * 16, [[1, 16]]), bank(bA, 16, o2), vcol("b1", 16 * l + uc),
                                    sgv(o2, 16), ALU.add, ALU.mult, ["sgm", "vecs"], [("u32s", uc), BK(bA)])
                            if a < H:
                                m = min(H, a + n) - a
                                TS(ATv(uc, a, m), ATv(uc, a, m), hm[:, 0:1], ALU.mult, ["hm"], R("at", uc, a, m))
                        W.release(q)
                for (src, nr, res, eoff, dst) in ((u32p, 30, "u32p", 0, ncp_o[l, :, :]), (u32s, 16, "u32s", 1024, ncs_o[l, :, 29, :])):
                    for half in range(2):
                        for cc in range(4):
                            c = 4 * half + cc
                            TR(V(pb[4 + half], 0, nr, cc * 128, [[1, 128]]), V(src, 0, 128, c * nr, [[1, nr]]), ident32[:],
                               [(res, c), "ident32"], [BK(4 + half)])
                        CP("act", UF(o_call, 0, nr, [[1, 512]], extra=eoff + half * 512), V(pb[4 + half], 0, nr, 0, [[1, 512]]),
                           [], [("cio", res, half), BK(4 + half)] + [("io", i_) for i_ in range(4)])
                    P.dma("sp", dst, UF(o_call, 0, nr, [[1, 1024]], extra=eoff), reads=[("cio", res, 0), ("cio", res, 1)])
                P.fence()
                g8all = [("g8", c) for c in range(NCH)]
                callall = [("call", c) for c in range(NCH)]
                for r in range(4):
                    P.dma("sp", UF(o_dwrep, 30 * r, 30, [[1, 1024]]), cdw[l, 0:30, :], writes=[("dwrep", r)])
                for t4 in range(4):
                    P.dma("sp", UF(o_call, 0, 120, [[1, 1024]], extra=t4 * 1024),
                          sconv[l, 4 * t4:4 * t4 + 4, :, :].rearrange("b k d -> (b k) d"), writes=[("st", t4)])
                for t4 in range(4):
                    TT(UF(o_call, 0, 120, [[1, 1024]], extra=t4 * 1024), UF(o_call, 0, 120, [[1, 1024]], extra=t4 * 1024),
                       UF(o_dwrep, 0, 120, [[1, 1024]]), ALU.mult, [("dwrep", r) for r in range(4)] + g8all, [("st", t4)])
                    for uc in range(NCH):
                        MM(bank(6, 4, uc * 16 + t4 * 4), UF(o_call, 0, 120, [[1, 128]], extra=t4 * 1024 + uc * 128), sel4[:],
                           uc == 0 and t4 == 0, True, [("st", t4), "sel4"] + callall, [BK(6)], skip=True)
                for uc in range(NCH):
                    ACTV(V(scs, 0, 128, uc * 16, [[1, 16]]), bank(6, 16, uc * 16), AF.Identity, ["vecs"], [("scs", uc), BK(6)],
                         bias=vcol("dwb", 8 * l + uc))
                g_ii = groups(lo_ii, T)

                def split(a, n):
                    ns = NS if a + n == T else 0
                    return n - ns, ns

                def conv_build(uc):
                    di = uc % 2
                    TT(V(U, 0, 128, o_dg[di], [[128, 31], [1, 128]]), V(identb, 0, 128, 0, [[0, 31], [1, 128]]),
                       V(vecs, 0, 128, VC["dwT"] + l * 248 + uc, [[8, 31], [0, 128]]), ALU.mult, ["identb", "vecs"], [("dg", di)])

                def conv_mm(a, n, uc, built=False):
                    np_, ns = split(a, n)
                    di = uc % 2
                    if not built:
                        conv_build(uc)
                    bi = uc % 2
                    for k in range(31):
                        MM(bank(bi, np_), V(U, 0, 128, o_dg[di] + k * 128, [[1, 128]]), ATv(uc, a - 30 + k, np_), k == 0, k == 30,
                           [("dg", di)] + R("at", uc, a - 30 + k, np_), [BK(bi)])
                    if ns:
                        MM(bank(6, 16, uc * 16), V(U, 0, 128, o_dg[di] + 30 * 128, [[1, 128]]), ATv(uc, TP, NS), uc == 0, True,
                           [("dg", di)] + R("at", uc, TP, NS), [BK(6)], skip=True)

                hidall = [("hid", i) for i in range(5)]

                def cl(par, uc, a0, n):
                    if uc == 0 and par == 1:
                        return sgv(a0, n)
                    return call(uc, a0, n)

                def clr(par, uc):
                    return hidall if (uc == 0 and par == 1) else [("call", uc)]

                def conv_evac(a, n, uc, par):
                    np_, ns = split(a, n)
                    bi = uc % 2
                    ACTV(cl(par, uc, 0, np_), bank(bi, np_), AF.Identity, ["vecs"], clr(par, uc) + [BK(bi)],
                         bias=vcol("dwb", 8 * l + uc))
                    if ns:
                        TT(cl(par, uc, np_, NS), bank(6, 16, uc * 16), V(scs, 0, 128, uc * 16, [[1, 16]]), ALU.add,
                           [("scs", uc)], clr(par, uc) + [BK(6)])

                def conv_sum(a, n, uc, par):
                    MM(bank(4, n), ones32[:], cl(par, uc, 0, n), uc == 0, uc == 7, clr(par, uc) + ["ones32"], [BK(4)])

                def conv_rest(a, n, par, first_uc, evac_done=False, nbuilt=0):
                    pending = []
                    for uc in range(first_uc):
                        if not evac_done:
                            conv_evac(a, n, uc, par)
                        pending.append(uc)
                    nb = first_uc + nbuilt
                    for uc in range(first_uc, NCH):
                        conv_mm(a, n, uc, built=(uc < nb))
                        nb = max(nb, uc + 1)
                        if nb < NCH and nb <= uc + 1:
                            conv_build(nb)
                            nb += 1
                        for p_ in pending:
                            conv_sum(a, n, p_, par)
                        pending = []
                        conv_evac(a, n, uc, par)
                        pending.append(uc)
                    for p_ in pending:
                        conv_sum(a, n, p_, par)

                def ln_stage(a, n, par, after_silu=None, mid=None):
                    TS(tmp(3, n), bank(4, n), -1.0 / D, ALU.mult, [], [("tmp", 3), BK(4)])
                    if mid is not None:
                        mid()
                    for uc in range(NCH):
                        TT(cl(par, uc, 0, n), cl(par, uc, 0, n), tmp(3, n), ALU.add, [("tmp", 3)], clr(par, uc))
                        ACTV(tmpb(uc % 2, n), cl(par, uc, 0, n), AF.Square, clr(par, uc), [("tmp", uc % 2)])
                        MM(bank(5, n), onesb[:], tmpb(uc % 2, n), uc == 0, uc == 7, [("tmp", uc % 2), "onesb"], [BK(5)])
                    RSTD(tmp(3, n), bank(5, n), 1.0 / D, 1e-5, [], [("tmp", 3), BK(5)])
                    for uc in range(NCH):
                        TT(cl(par, uc, 0, n), cl(par, uc, 0, n), tmp(3, n), ALU.mult, [("tmp", 3)], clr(par, uc))
                        ACTV(g8(uc, n), cl(par, uc, 0, n), AF.Silu, clr(par, uc) + ["vecs"], [("g8", uc)],
                             bias=vcol("lnb", 8 * l + uc), scale=vcol("lng", 8 * l + uc))
                        if after_silu is not None:
                            after_silu(uc)

                def w2_stage(a, n):
                    for i in range(2):
                        q, s = W.acquire(pid_w2(l, i))
                        for cc in range(4):
                            oc = 4 * i + cc
                            bi = 2 + oc % 2
                            for kc in range(NCH):
                                MM(bank(bi, n), Wv(s, kc * 512 + cc * 128, 128), g8(kc, n), kc == 0, kc == NCH - 1,
                                   [("w", s), ("g8", kc)], [BK(bi)])
                            STT(xTv(oc, a, n), bank(bi, n), vcol("b2", 8 * l + oc), xTv(oc, a, n), ALU.add, ALU.add,
                                ["vecs"], R("x", oc, a, n) + [BK(bi)])
                        W.release(q)

                NPRE = 2
                conv_rest(g_ii[0][0], g_ii[0][1], 0, 0)
                prebuilt = False
                for gidx, (a, n) in enumerate(g_ii):
                    par = gidx % 2
                    nxt = g_ii[gidx + 1] if gidx + 1 < len(g_ii) else None
                    if nxt:
                        for uc in range(NPRE):
                            conv_mm(nxt[0], nxt[1], uc, built=prebuilt)
                        conv_evac(nxt[0], nxt[1], 0, 1 - par)
                        ln_stage(a, n, par,
                                 lambda uc, nxt=nxt, par=par: conv_evac(nxt[0], nxt[1], uc, 1 - par) if uc == 1 else None,
                                 mid=lambda: conv_build(NPRE))
                        conv_rest(nxt[0], nxt[1], 1 - par, NPRE, evac_done=True, nbuilt=1)
                    else:
                        ln_stage(a, n, par)
                    w2_stage(a, n)
                    if gidx + 2 < len(g_ii):
                        conv_build(0)
                        conv_build(1)
                        prebuilt = True
                    else:
                        prebuilt = False
                    mlp_norm(l, a, n)
                kvg = groups(64, T)

                def kv_norm_hook(e8, gi):
                    if l == 1 and e8 == 7:
                        a_, n_ = kvg[gi]
                        norm_group("kvn", 0, a_, n_, lambda c, a_=a_, n_=n_: ATv(c, a_, n_), lambda c, a_=a_, n_=n_: R("at", c, a_, n_))

                mlp(l, groups(lo_ii, T), do_norm=False, after_unit=kv_norm_hook)
                P.fence()

            P.dma("pool", V(U, 0, 128, o_E, [[1, 4096]]), emask_d[:], writes=["E"])
            kvg = groups(64, T)
            P.op("dve", lambda e: e.memset(V(U, 0, 128, o_KT, [[1, 4 * KVW]]), 0.0), (), ["KTzero"])
            q, s = W.acquire(PID_KV)
            ki = 0
            for t in range(2):
                for (a, n) in kvg:
                    np_ = min(a + n, TP) - a
                    bi = ki % 2
                    bs = 2 + ki % 2
                    ki += 1
                    for kc in range(NCH):
                        MM(bank(bi, np_), Wv(s, kc * 512 + t * 128, 128), ATv(kc, a, np_), kc == 0, kc == NCH - 1,
                           [("w", s)] + R("at", kc, a, np_), [BK(bi)])
                    ACTV(tmpb(bi, np_), bank(bi, np_), AF.Square, [], [("tmp", bi), BK(bi)])
                    MM(bank(bs, np_), bd64b[:], tmpb(bi, np_), True, True, [("tmp", bi), "bd64b"], [BK(bs)])
                    RSTD(tmp(3, np_), bank(bs, np_), 1.0 / 64, 1e-6, [], [("tmp", 3), BK(bs)])
                    for hf_ in range(2):
                        STT(V(U, hf_ * 64, 64, o_KT + (2 * t + hf_) * KVW + a - 64, [[1, np_]]), V(pb[bi], hf_ * 64, 64, 0, [[1, np_]]),
                            V(vecs, hf_ * 64, 64, VC["kng"], [[1, 1]]), UF(o_tmp[3], hf_ * 64, 64, [[1, np_]]),
                            ALU.mult, ALU.mult, [("tmp", 3), "vecs", "KTzero"], [("KT", t, a, hf_), BK(bi)])
                    if a <= TP - 128 and a + np_ >= TP:
                        o1 = TP - 128 - a
                        STT(V(K32, 0, 128, t * 128, [[1, 128]]), bank(bi, 128, o1), vcol("kng", 0),
                            UF(o_tmp[3], 0, 128, [[1, 128]], extra=o1), ALU.mult, ALU.mult,
                            [("tmp", 3), "vecs"], [("K32", t), BK(bi)])
            P.op("dve", lambda e: e.memset(V(U, 0, 128, o_Vtm + 64, [[65, NKB * 4], [1, 1]]), 1.0), (), ["Vones"])
            for blk in range(NKB):
                a = 64 + blk * 128
                bi = 4 + blk % 2
                for kc in range(NCH):
                    MM(bank(bi, 256), ATv(kc, a, 128), Wv(s, kc * 512 + 256, 256), kc == 0, kc == NCH - 1,
                       [("w", s)] + R("at", kc, a, 128), [BK(bi)])
                CP("act", V(U, 0, 128, o_Vtm + blk * 260, [[65, 4], [1, 64]]), V(pb[bi], 0, 128, 0, [[64, 4], [1, 64]]),
                   [], [("Vtm", blk), BK(bi)])
                if blk == NKB - 1:
                    CP("dve", tmp(0, 256), bank(bi, 256), [], [("tmp", 0), BK(bi)])
                    P.dma("sp", nvp_o[:, :], tmp(0, 256), reads=[("tmp", 0)], writes=[("tmp", 0)])
            TS(V(U, 0, 128, o_Vtm, [[1, 260]]), V(U, 0, 128, o_Vtm, [[1, 260]]), hm[:, 0:1], ALU.mult,
               ["hm"], [("Vtm", 0), "Vones"])
            for kc in range(NCH):
                MM(bank(6, 512, 0, 0, 16), ATv(kc, TP, NS), Wv(s, kc * 512, 512), kc == 0, kc == NCH - 1,
                   [("w", s)] + R("at", kc, TP, NS), [BK(6)])
            CP("act", kvs[:], bank(6, 512, 0, 0, 16), [], ["kvs", BK(6)])
            TT(tmp(1, 256, 0, 16), kvs[:, 0:256], kvs[:, 0:256], ALU.mult, ["kvs"], [("tmp", 1)])
            RED(rk[:], UF(o_tmp[1], 0, 16, [[64, 4], [1, 64]]), [("tmp", 1)], ["rk"])
            RSTD(rk[:], rk[:], 1.0 / 64, 1e-6, [], ["rk"])
            TT(V(knew, 0, 16, 0, [[64, 4], [1, 64]]), V(kvs, 0, 16, 0, [[64, 4], [1, 64]]), V(rk, 0, 16, 0, [[1, 4], [0, 64]]),
               ALU.mult, ["kvs", "rk"], ["knew"])
            TT(V(knew, 0, 16, 0, [[64, 4], [1, 64]]), V(knew, 0, 16, 0, [[64, 4], [1, 64]]), V(gtm, 0, 16, 0, [[0, 4], [1, 64]]),
               ALU.mult, ["gtm"], ["knew"])
            P.dma("sp", nks_o[:, 127, :], knew[:], reads=["knew"])
            P.dma("sp", nvs_o[:, 127, :], kvs[:, 256:512], reads=["kvs"])
            for t in range(2):
                for kc in range(NCH):
                    MM(bank(7, 16, t * 16), Wv(s, kc * 512 + 256 + t * 128, 128), ATv(kc, TP, NS), kc == 0, kc == NCH - 1,
                       [("w", s)] + R("at", kc, TP, NS), [BK(7)], skip=True)
            CP("act", VnT[:], bank(7, 32), [], ["VnT", BK(7)])
            W.release(q)
            for t in range(2):
                TR(bank(0, 128, t * 128), V(K32, 0, 128, t * 128, [[1, 128]]), ident32[:], [("K32", t), "ident32"], [BK(0)])
            CP("dve", tmp(1, 256), bank(0, 256), [], [("tmp", 1), BK(0)])
            P.dma("sp", nkp_o[:, :], tmp(1, 256), reads=[("tmp", 1)], writes=[("tmp", 1)])
            P.fence()

            def ogv(c, a, n):
                return V(AT, 0, 128, o_og + c * 512 + a, [[1, n]])

            def wo_group(l, a, n):
                for i in range(2):
                    q, s = W.acquire(pid_wo(l, i))
                    for cc in range(4):
                        oc = 4 * i + cc
                        bi = oc % 2
                        for kc in range(NCH):
                            MM(bank(bi, n), Wv(s, kc * 512 + cc * 128, 128), ogv(kc, 0, n), kc == 0, kc == NCH - 1,
                               [("w", s), ("og", kc)], [BK(bi)])
                        TT(xTv(oc, a, n), bank(bi, n), xTv(oc, a, n), ALU.add, [], R("x", oc, a, n) + [BK(bi)])
                    W.release(q)

            for l in (2, 3):
                lb = l - 2
                g_b = [(H + 512 * gi, 512) for gi in range(4)]
                norm_stats(g_b[0][0], 512, 0)
                for gi in range(4):
                    a, n = g_b[gi]
                    norm_apply("nmix", 8 * l, a, n, lambda c, n=n: g8(c, n), lambda c: [("g8", c)], gi % 2)
                    if gi + 1 < 4:
                        norm_stats(g_b[gi + 1][0], 512, (gi + 1) % 2)
                    QB = (0, 1, 4, 5)

                    def qfin(qc):
                        ti = qc % 2
                        bi = QB[qc % 4]
                        bs = 2 + qc % 2
                        MM(bank(bs, n), bd64b[:], tmpb(ti, n), True, True, [("tmp", ti), "bd64b"], [BK(bs)])
                        RSTD(sgv(ti * 512, n), bank(bs, n), 1.0 / 64, 1e-6, [], [("qr", ti), BK(bs)])
                        STT(V(AT, 0, 128, o_qg + qc * 512, [[1, n]]), bank(bi, n), qng8[:, lb:lb + 1], sgv(ti * 512, n),
                            ALU.mult, ALU.mult, [("qr", ti), "qng8"], [("qg", qc), BK(bi)])

                    pend = None
                    for i in range(2):
                        q, s = W.acquire(pid_wq(l, i))
                        for cc in range(4):
                            qc = 4 * i + cc
                            bi = QB[qc % 4]
                            for kc in range(NCH):
                                MM(bank(bi, n), Wv(s, kc * 512 + cc * 128, 128), g8(kc, n), kc == 0, kc == NCH - 1,
                                   [("w", s), ("g8", kc)], [BK(bi)])
                            ACTV(tmpb(qc % 2, n), bank(bi, n), AF.Square, [], [("tmp", qc % 2), BK(bi)])
                            if pend is not None:
                                qfin(pend)
                            pend = qc
                        W.release(q)
                    qfin(pend)

                    def stageA(qb, j):
                        pt = (4 * qb + j) % 3
                        Bk = 4 * gi + qb
                        t, hf = j // 2, j % 2
                        r0 = hf * 64
                        cs = (j // 2) * 4
                        pj = j % 2
                        bS = (0, 1) if pj == 0 else (2, 3)
                        for kb in range(2):
                            MM(bank(bS[kb], 512), V(U, 0, 128, o_KT + j * KVW + (Bk + kb) * 128, [[1, 128]]),
                               V(AT, 0, 128, o_qg + cs * 512 + qb * 128, [[512, 4], [1, 128]]), True, True,
                               ["KT"] + [("qg", cs + i) for i in range(4)], [BK(bS[kb])])
                            ACTV(V(AT, 0, 128, o_P0[pj] + kb * 512, [[1, 512]]), bank(bS[kb], 512), AF.Exp,
                                 [], [("P0", pj, kb), BK(bS[kb])])
                        TT(V(AT, 0, 128, o_PT[pt], [[1, 1024]]), V(AT, 0, 128, o_P0[pj], [[1, 1024]]),
                           V(U, 0, 128, o_E + j * 1024, [[1, 1024]]), ALU.mult,
                           [("P0", pj, 0), ("P0", pj, 1), "E"], [("PT", pt, 0), ("PT", pt, 1)])

                    def stageB(qb, j):
                        pt = (4 * qb + j) % 3
                        Bk = 4 * gi + qb
                        hf = j % 2
                        cs = (j // 2) * 4
                        pj = j % 2
                        bO = 4 + pj
                        for i in range(4):
                            for kb in range(2):
                                MM(V(pb[bO], 0, 128, i * 65, [[1, 65]]), V(AT, 0, 128, o_PT[pt] + kb * 512 + i * 128, [[1, 128]]),
                                   V(U, 0, 128, o_Vtm + ((Bk + kb) * 4 + j) * 65, [[1, 65]]), kb == 0, kb == 1,
                                   ["Vtm", "Vones", ("PT", pt, kb)], [BK(bO)], skip=True)
                        dj = V(dnt, 0, 128, pj * 4, [[1, 4]])
                        TT(dj, V(pb[bO], 0, 128, 64, [[65, 4]]), V(esink, 0, 128, lb * 16 + 2 * cs + hf, [[2, 4]]), ALU.add,
                           ["esink"], [("dnt", pj), BK(bO)])
                        RCP(dj, dj, [], [("dnt", pj)])
                        og_i = Bk % 2
                        TT(V(AT, 0, 128, o_ogt[og_i] + j * 256, [[64, 4], [1, 64]]), V(pb[bO], 0, 128, 0, [[65, 4], [1, 64]]),
                           V(dnt, 0, 128, pj * 4, [[1, 4], [0, 64]]), ALU.mult,
                           [("dnt", pj)], [("ogt", og_i, j), BK(bO)])

                    def stageC(qb):
                        Bk = 4 * gi + qb
                        og_i = Bk % 2
                        bT = 6 + Bk % 2
                        pTv = V(pb[bT], 0, 128, 0, [[1, 512]]).bitcast(BF16)
                        for c in range(NCH):
                            TR(pTv[:, c * 128:(c + 1) * 128], V(AT, 0, 128, o_ogt[og_i] + c * 128, [[1, 128]]), identb[:],
                               [("ogt", og_i, c // 2), "identb"], [BK(bT)])
                        CP("act", V(AT, 0, 128, o_og + qb * 128, [[512, 4], [1, 128]]), pTv[:, 0:512].rearrange("p (c q) -> p c q", q=128),
                           [], [("og", c) for c in range(4)] + [BK(bT)])
                        CP("act", V(AT, 0, 128, o_og + 4 * 512 + qb * 128, [[512, 4], [1, 128]]),
                           pTv[:, 512:1024].rearrange("p (c q) -> p c q", q=128),
                           [], [("og", c) for c in range(4, 8)] + [BK(bT)])

                    items = [(qb, j) for qb in range(4) for j in range(4)]
                    LAG = 2
                    for k in range(min(LAG, len(items))):
                        stageA(*items[k])
                    pendC = None
                    for k in range(len(items)):
                        if k + LAG < len(items):
                            stageA(*items[k + LAG])
                        stageB(*items[k])
                        if pendC is not None:
                            stageC(pendC)
                            pendC = None
                        if items[k][1] == 3:
                            pendC = items[k][0]
                    if pendC is not None:
                        stageC(pendC)
                    wo_group(l, a, n)
                P.fence()
                a, n = TP, NS
                norm_group("nmix", 8 * l, a, n, lambda c, n=n: g8(c, n), lambda c: [("g8", c)])
                for i in range(2):
                    q, s = W.acquire(pid_wq(l, i))
                    for kc in range(NCH):
                        MM(bank(i, 512, 0, 0, 16), g8(kc, NS), Wv(s, kc * 512, 512), kc == 0, kc == NCH - 1,
                           [("w", s), ("g8", kc)], [BK(i)])
                    CP("act", AF32(o_qtm, 0, 16, [[1, 512]], extra=i * 512), bank(i, 512, 0, 0, 16), [], ["qtm", BK(i)])
                    W.release(q)
                qt3 = AF32(o_qtm, 0, 16, [[64, 16], [1, 64]])
                pr3 = AF32(o_prod, 0, 16, [[64, 16], [1, 64]])
                TT(AF32(o_prod, 0, 16, [[1, 1024]]), AF32(o_qtm, 0, 16, [[1, 1024]]), AF32(o_qtm, 0, 16, [[1, 1024]]), ALU.mult,
                   ["qtm"], ["prod"])
                RED(sn[:], pr3, ["prod"], ["sn"])
                RSTD(rq[:], sn[:], 1.0 / 64, 1e-6, ["sn"], ["rq"])
                TT(qt3, qt3, V(rq, 0, 16, 0, [[1, 16], [0, 64]]), ALU.mult, ["rq"], ["qtm"])
                STT(qt3, qt3, 0.125, V(gtm, 0, 16, 64 * (1 + lb), [[0, 16], [1, 64]]), ALU.mult, ALU.mult, ["gtm"], ["qtm"])
                for j in range(4):
                    hf, cs = j % 2, (j // 2) * 4
                    p0 = 2 * cs + hf
                    TT(AF32(o_prod, 0, 16, [[128, 4], [1, 64]], extra=p0 * 64), AF32(o_qtm, 0, 16, [[128, 4], [1, 64]], extra=p0 * 64),
                       V(knew, 0, 16, j * 64, [[0, 4], [1, 64]]), ALU.mult, ["qtm", "knew"], ["prod"])
                RED(sn[:], pr3, ["prod"], ["sn"])
                ACTV(sn[:], sn[:], AF.Exp, [], ["sn"])
                for c in range(NCH):
                    TR(V(pb[2], 0, 128, c * 16, [[1, 16]]), AF32(o_qtm, 0, 16, [[1, 128]], extra=c * 128), V(ident32, 0, 16, 0, [[1, 16]]),
                       ["qtm", "ident32"], [BK(2)])
                CP("act", V(AT, 0, 128, o_QsT, [[1, 128]]), bank(2, 128), [], ["QsT", BK(2)])
                for b in range(NS):
                    k3 = b % 2
                    P.dma("sp", AF32(o_kst[k3], 0, 128, [[1, 256]]), swk[b, :, :], writes=[("kst", k3)])
                    bt = 3 if b % 2 == 0 else 7
                    for t in range(2):
                        TR(bank(bt, 128, t * 128), AF32(o_kst[k3], 0, 128, [[1, 128]], extra=t * 128), ident32[:],
                           [("kst", k3), "ident32"], [BK(bt)])
                    CP("act" if b % 2 == 0 else "dve", V(AT, 0, 128, o_KsT[b % 2], [[1, 256]]), bank(bt, 256), [],
                       [("KsT", b % 2), BK(bt)])
                    for j in range(4):
                        t, hf = j // 2, j % 2
                        r0 = hf * 64
                        cs = (j // 2) * 4
                        MM(V(pb[6], 0, 128, b * 16 + 2 * cs + hf, [[2, 4]]), V(AT, r0, 64, o_KsT[b % 2] + t * 128, [[1, 128]]),
                           V(AT, r0, 64, o_QsT + cs * 16 + b, [[16, 4]]), b == 0 and j == 0, True,
                           [("KsT", b % 2), "QsT"], [BK(6)], skip=True)
                TT(AF32(o_ssa, 0, 128, [[16, 16], [1, 16]]), V(pb[6], 0, 128, 0, [[16, 16], [1, 16]]), V(ebs, 0, 128, 0, [[0, 16], [1, 16]]),
                   ALU.add, ["ebs"], ["ssa", BK(6)])
                ACTV(AF32(o_pba, 0, 128, [[1, 256]]), AF32(o_ssa, 0, 128, [[1, 256]]), AF.Exp, ["ssa"], ["pba"])
                for b in range(NS):
                    k3 = b % 2
                    P.dma("sp", AF32(o_vst[k3], 0, 128, [[1, 256]]), swv[b, :, :], writes=[("vst", k3)])
                    for t in range(2):
                        MM(V(pb[4], 0, 128, t * 128 + b, [[16, 8]]), AF32(o_vst[k3], 0, 128, [[1, 128]], extra=t * 128),
                           AF32(o_pba, 0, 128, [[1, 8]], extra=b * 16 + t * 8), b == 0 and t == 0, True,
                           [("vst", k3), "pba"], [BK(4)], skip=True)
                MM(bank(5, 256), ones32[:], AF32(o_pba, 0, 128, [[1, 256]]), True, True, ["pba", "ones32"], [BK(5)])
                TT(AF32(o_dD, 0, 16, [[16, 16], [1, 16]]), V(sn, 0, 16, 0, [[1, 16], [0, 16]]), V(i16, 0, 16, 0, [[0, 16], [1, 16]]),
                   ALU.mult, ["sn", "i16"], ["dD"])
                MM(bank(6, 256), ones32[0:16, :], AF32(o_dD, 0, 16, [[1, 256]]), True, True, ["dD", "ones32"], [BK(6)])
                CP("act", AF32(o_pnb, 0, 128, [[1, 256]]), bank(6, 256), [], ["pnb", BK(6)])
                TT(AF32(o_on, 0, 128, [[128, 2], [16, 8], [1, 16]]), AF32(o_pnb, 0, 128, [[128, 2], [16, 8], [1, 16]]),
                   V(VnT, 0, 128, 0, [[16, 2], [0, 8], [1, 16]]), ALU.mult, ["pnb", "VnT"], ["on"])
                TT(AF32(o_on, 0, 128, [[1, 256]]), bank(4, 256), AF32(o_on, 0, 128, [[1, 256]]), ALU.add, [], ["on", BK(4)])
                TT(AF32(o_dt, 0, 128, [[16, 16], [1, 16]]), V(pb[5], 0, 128, 0, [[1, 16], [16, 16]]),
                   AF32(o_pnb, 0, 128, [[16, 16], [1, 16]]), ALU.add, ["pnb"], ["dt", BK(5)])
                TT(AF32(o_dt, 0, 128, [[16, 16], [1, 16]]), AF32(o_dt, 0, 128, [[16, 16], [1, 16]]),
                   V(esink, 0, 128, lb * 16, [[1, 16], [0, 16]]), ALU.add, ["esink"], ["dt"])
                RCP(AF32(o_dt, 0, 128, [[1, 256]]), AF32(o_dt, 0, 128, [[1, 256]]), [], ["dt"])
                TT(AF32(o_on, 0, 128, [[1, 256]]), AF32(o_on, 0, 128, [[1, 256]]), AF32(o_dt, 0, 128, [[1, 256]]), ALU.mult,
                   ["dt"], ["on"])
                MM(bank(7, 256), swp[:], AF32(o_on, 0, 128, [[1, 256]]), True, True, ["on", "swp"], [BK(7)])
                for h in range(16):
                    p = PERM.index(h)
                    sh, dh = p % 2, h % 2
                    dst = V(AT, dh * 64, 64, o_og + (h // 2) * 512, [[1, 16]])
                    if sh == dh:
                        CP("dve", dst, AF32(o_on, dh * 64, 64, [[1, 16]], extra=p * 16), ["on"], [("og", h // 2)])
                    else:
                        CP("dve", dst, V(pb[7], dh * 64, 64, p * 16, [[1, 16]]), [], [("og", h // 2), BK(7)])
                wo_group(l, a, n)
                P.fence()
                mlp(l, groups(H, T))
                P.fence()


        dry = Ring(None, None, None, None)
        emit_all(DryProg(), dry)
        P = Prog(nc, st)
        W = Ring(P, Wt, wst, dry.log)
        emit_all(P, W)
        assert W.nacq == len(dry.log)
        P.emit()
    return nc


def _pieces(conv_w1, conv_w2, mlp_w1, mlp_w2, w_kv, attn_wq, attn_wo):
    wst = np.empty((NPIECE, 128, SLOT), np.float32)

    def k8(Wsub):
        return Wsub.reshape(8, 128, 512).transpose(1, 0, 2).reshape(128, SLOT)

    def j4(Wsub):
        return Wsub.reshape(4, 128, 1024).transpose(1, 0, 2).reshape(128, SLOT)

    for l in range(2):
        for i in range(4):
            sub = np.concatenate([conv_w1[l][:, i * 256:(i + 1) * 256], conv_w1[l][:, 1024 + i * 256:1024 + (i + 1) * 256]], axis=1)
            wst[pid_w1(l, i)] = k8(sub)
        for i in range(2):
            wst[pid_w2(l, i)] = k8(conv_w2[l][:, i * 512:(i + 1) * 512])
    for l in range(4):
        for e in range(8):
            wst[pid_mlp(l, e, 0)] = k8(mlp_w1[l][:, e * 512:(e + 1) * 512])
            wst[pid_mlp(l, e, 1)] = j4(mlp_w2[l][e * 512:(e + 1) * 512, :])
    wst[PID_KV] = k8(w_kv)
    colperm = np.concatenate([np.arange(PERM[p] * 64, PERM[p] * 64 + 64) for p in range(16)])
    for l in (2, 3):
        wqp = attn_wq[l - 2][:, colperm]
        wop = attn_wo[l - 2]
        for i in range(2):
            wst[pid_wq(l, i)] = k8(wqp[:, i * 512:(i + 1) * 512])
            wst[pid_wo(l, i)] = k8(wop[:, i * 512:(i + 1) * 512])
    return wst


def _fm(v):
    return np.ascontiguousarray(v.reshape(8, 128).T)


_NC_CACHE = {}


def kernel(x_prompt, x_sample, state_conv, state_win_k, state_win_v,
           norm_mix_g, norm_mlp_g,
           conv_w1, conv_b1, conv_dw, conv_dwb, conv_ln_g, conv_ln_b, conv_w2, conv_b2,
           kv_norm_g, w_kv, k_norm_g,
           attn_wq, q_norm_g, attn_sinks, attn_wo,
           mlp_w1, mlp_w2):
    f = lambda a: np.ascontiguousarray(np.asarray(a, dtype=np.float32))
    x_prompt, x_sample, state_conv, state_win_k, state_win_v = map(f, (x_prompt, x_sample, state_conv, state_win_k, state_win_v))
    norm_mix_g, norm_mlp_g, conv_w1, conv_b1, conv_dw, conv_dwb = map(f, (norm_mix_g, norm_mlp_g, conv_w1, conv_b1, conv_dw, conv_dwb))
    conv_ln_g, conv_ln_b, conv_w2, conv_b2, kv_norm_g, w_kv, k_norm_g = map(f, (conv_ln_g, conv_ln_b, conv_w2, conv_b2, kv_norm_g, w_kv, k_norm_g))
    attn_wq, q_norm_g, attn_sinks, attn_wo, mlp_w1, mlp_w2 = map(f, (attn_wq, q_norm_g, attn_sinks, attn_wo, mlp_w1, mlp_w2))

    vecs = np.zeros((128, NV), np.float32)
    for l in range(4):
        vecs[:, VC["nmix"] + 8 * l:VC["nmix"] + 8 * l + 8] = _fm(norm_mix_g[l])
        vecs[:, VC["nmlp"] + 8 * l:VC["nmlp"] + 8 * l + 8] = _fm(norm_mlp_g[l])
    vecs[:, VC["kvn"]:VC["kvn"] + 8] = _fm(kv_norm_g)
    for l in range(2):
        vecs[:, VC["b1"] + 16 * l:VC["b1"] + 16 * l + 16] = conv_b1[l].reshape(16, 128).T
        vecs[:, VC["dwb"] + 8 * l:VC["dwb"] + 8 * l + 8] = _fm(conv_dwb[l])
        vecs[:, VC["lng"] + 8 * l:VC["lng"] + 8 * l + 8] = _fm(conv_ln_g[l])
        vecs[:, VC["lnb"] + 8 * l:VC["lnb"] + 8 * l + 8] = _fm(conv_ln_b[l])
        vecs[:, VC["b2"] + 8 * l:VC["b2"] + 8 * l + 8] = _fm(conv_b2[l])
        vecs[:, VC["dwT"] + 248 * l:VC["dwT"] + 248 * (l + 1)] = conv_dw[l].reshape(31, 8, 128).transpose(2, 0, 1).reshape(128, 248)
    vecs[:, VC["kng"]] = np.tile(k_norm_g, 2)
    for lb in range(2):
        vecs[:, VC["qng"] + lb] = np.tile(q_norm_g[lb], 2)
        vecs[:, VC["sink"] + 16 * lb:VC["sink"] + 16 * lb + 16] = np.broadcast_to(attn_sinks[lb][PERM], (128, 16))
    gtm = np.ascontiguousarray(np.broadcast_to(np.concatenate([k_norm_g, q_norm_g[0], q_norm_g[1]])[None, :], (NS, 192)))
    wst = _pieces(conv_w1, conv_w2, mlp_w1, mlp_w2, w_kv, attn_wq, attn_wo)

    ident = np.eye(128, dtype=np.float32)
    bd64 = np.zeros((128, 128), np.float32)
    bd64[:64, :64] = 1.0
    bd64[64:, 64:] = 1.0
    sel4 = np.zeros((120, 4), np.float32)
    for b in range(4):
        sel4[30 * b:30 * b + 30, b] = 1.0
    i16 = np.eye(16, dtype=np.float32)
    swp = np.zeros((128, 128), np.float32)
    for kk in range(128):
        swp[kk, (kk + 64) % 128] = 1.0
    s_ = np.arange(128)[:, None, None, None, None]
    kb_ = np.arange(2)[None, None, :, None, None]
    q_ = np.arange(128)[None, None, None, None, :]
    delta = 128 + q_ - (kb_ * 128 + s_)
    slope = np.array(SLOPES, np.float64).reshape(4, 4)[None, :, None, :, None]
    valid = (delta >= 0) & (delta <= 128)
    emask = np.where(valid, np.exp(-slope * delta), 0.0).astype(np.float32).reshape(128, 4096)
    ebs = np.zeros((128, 16), np.float32)
    for p in range(16):
        ebs[:, p] = -SLOPES[PERM[p]] * (128 - np.arange(128))

    in_maps = []
    for core in range(8):
        b, r = core // 4, core % 4
        xin = np.zeros((TP, D), np.float32)
        if r == 0:
            xin[H:] = x_prompt[b, 0:NO]
        else:
            xin[:] = x_prompt[b, r * NO - H:(r + 1) * NO]
        sl = slice(core * NS, (core + 1) * NS)
        hm = np.zeros((128, 2), np.float32)
        hm[:, 0] = 0.0 if r == 0 else 1.0
        in_maps.append({
            "xin": xin, "xs": np.ascontiguousarray(x_sample[sl, 0, :]),
            "sconv": np.ascontiguousarray(state_conv[:, sl]),
            "swk": np.ascontiguousarray(state_win_k[sl].reshape(NS, 128, 256)),
            "swv": np.ascontiguousarray(state_win_v[sl].reshape(NS, 128, 256)),
            "hm": hm, "vecs": vecs, "gtm": gtm, "cdw": conv_dw, "wst": wst,
            "ident": ident, "bd64": bd64, "sel4": sel4, "i16": i16, "emask": emask, "ebs": ebs, "swp": swp,
        })
    if "nc" not in _NC_CACHE:
        _NC_CACHE["nc"] = build_program()
    nc = _NC_CACHE["nc"]
    res = run_bass_kernel_spmd(nc, in_maps, core_ids=list(range(8)))
    rs = res.results

    y_prompt = np.empty((2, 8192, D), np.float32)
    y_sample = np.empty((128, 1, D), np.float32)
    ncp = np.empty((2, 2, 30, D), np.float32)
    ncs = np.empty((2, 128, 30, D), np.float32)
    nkp = np.empty((2, 128, 4, 64), np.float32)
    nvp = np.empty((2, 128, 4, 64), np.float32)
    nks = np.empty((128, 128, 4, 64), np.float32)
    nvs = np.empty((128, 128, 4, 64), np.float32)
    for core in range(8):
        b, r = core // 4, core % 4
        o = rs[core]
        sl = slice(core * NS, (core + 1) * NS)
        y_prompt[b, r * NO:(r + 1) * NO] = o["y"][:NO]
        y_sample[sl, 0] = o["y"][NO:]
        ncs[:, sl] = o["ncs"]
        nks[sl] = o["nks"].reshape(NS, 128, 4, 64)
        nvs[sl] = o["nvs"].reshape(NS, 128, 4, 64)
        if r == 3:
            ncp[:, b] = o["ncp"]
            nkp[b] = o["nkp"].reshape(128, 4, 64)
            nvp[b] = o["nvp"].reshape(128, 4, 64)
    return (y_prompt, y_sample, ncp, ncs, nkp, nvp, nks, nvs)
```

```python
from contextlib import ExitStack
import numpy as np
import concourse.bass as bass
import concourse.mybir as mybir
from concourse.bass_utils import run_bass_kernel_spmd

F32 = mybir.dt.float32
BF16 = mybir.dt.bfloat16
ALU = mybir.AluOpType
AF = mybir.ActivationFunctionType
AX = mybir.AxisListType

D = 1024
NCH = 8
H = 192
NO = 2048
NS = 16
TP = H + NO
T = TP + NS
NSLOT = 4
SLOT = 4096
NKB = 17
KVW = NKB * 128
N_HEADS = 16
COMPUTE = ("pe", "act", "dve", "pool")
NDMASEM = 8

VC = {}
_c = 0
for _n, _w in (("nmix", 32), ("nmlp", 32), ("kvn", 8), ("b1", 32), ("dwb", 16), ("lng", 16),
               ("lnb", 16), ("b2", 16), ("dwT", 2 * 31 * 8), ("kng", 1), ("qng", 2), ("sink", 32)):
    VC[_n] = _c
    _c += _w
NV = _c


def perm_head(p):
    c, hf = p // 2, p % 2
    if c < 4:
        return c if hf == 0 else 4 + c
    return 8 + (c - 4) if hf == 0 else 12 + (c - 4)


PERM = [perm_head(p) for p in range(16)]
SLOPES = [2.0 ** (-8.0 * (h + 1) / 16.0) for h in range(16)]


class Op:
    __slots__ = ("eng", "fn", "deps", "idx", "signal", "cnt", "dma", "dslot", "dcnt", "prevdma")

    def __init__(self, eng, fn, dma):
        self.eng = eng
        self.fn = fn
        self.deps = []
        self.signal = False
        self.cnt = 0
        self.dma = dma
        self.prevdma = None


class Prog:
    def __init__(self, nc, stack):
        self.nc = nc
        self.eobj = {"pe": nc.tensor, "act": nc.scalar, "dve": nc.vector,
                     "pool": nc.gpsimd, "sp": nc.sync}
        self.ops = {e: [] for e in self.eobj}
        self.order = []
        self.lastw = {}
        self.readers = {}
        self.sem = {e: stack.enter_context(nc.semaphore("s_" + e)) for e in COMPUTE}
        self.dsem = {e: [stack.enter_context(nc.semaphore("d_%s%d" % (e, i))) for i in range(NDMASEM)]
                     for e in ("sp", "pool")}
        self.dmaops = {e: [] for e in self.eobj}
        self.pending = {}

    def fence(self):
        fs = set()
        for e, lst in self.ops.items():
            comp = [o for o in lst if not o.dma]
            if comp:
                fs.add(comp[-1])
            if e != "pool":
                for o in self.dmaops[e][-NDMASEM:]:
                    fs.add(o)
        for e in self.eobj:
            self.pending[e] = set(fs) | self.pending.get(e, set())

    def op(self, eng, fn, reads=(), writes=(), dma=False, nofence=False):
        o = Op(eng, fn, dma)
        deps = set()
        if eng in self.pending and not nofence:
            deps |= self.pending.pop(eng)
        for r in reads:
            w = self.lastw.get(r)
            if w is not None:
                deps.add(w)
        for r in writes:
            w = self.lastw.get(r)
            if w is not None:
                deps.add(w)
            rl = self.readers.get(r)
            if rl:
                for kk, vv in rl.items():
                    if kk == "dma":
                        deps.update(vv)
                    else:
                        deps.add(vv)
        o.deps = list(deps)
        for r in reads:
            rd = self.readers.setdefault(r, {})
            if dma:
                rd.setdefault("dma", []).append(o)
            else:
                rd[eng] = o
        for r in writes:
            self.lastw[r] = o
            self.readers[r] = {}
        o.idx = len(self.ops[eng])
        self.ops[eng].append(o)
        self.order.append(o)
        if dma:
            q = self.dmaops[eng]
            o.dslot = len(q) % NDMASEM
            o.dcnt = 16 * (len(q) // NDMASEM + 1)
            o.prevdma = q[-NDMASEM] if len(q) >= NDMASEM else None
            q.append(o)
        return o

    def pe(self, fn, reads=(), writes=()):
        return self.op("pe", fn, reads, writes)

    def act(self, fn, reads=(), writes=()):
        return self.op("act", fn, reads, writes)

    def dve(self, fn, reads=(), writes=()):
        return self.op("dve", fn, reads, writes)

    def dma(self, eng, out, in_, reads=(), writes=(), nofence=False):
        return self.op(eng, lambda e: e.dma_start(out=out, in_=in_), reads, writes, dma=True, nofence=nofence)

    def emit(self):
        for o in self.order:
            for d in o.deps:
                if d.dma:
                    continue
                if d.eng == "pe" and o.eng == "pe" and not o.dma:
                    continue
                d.signal = True
        for e in COMPUTE:
            c = 0
            for o in self.ops[e]:
                if o.signal and not o.dma:
                    c += 1
                    o.cnt = c
        for e, lst in self.ops.items():
            eng = self.eobj[e]
            waited = {}
            for o in lst:
                need = {}
                for d in o.deps:
                    if d.dma:
                        key = ("d", d.eng, d.dslot)
                        val = d.dcnt
                    else:
                        if d.eng == "pe" and e == "pe" and not o.dma:
                            continue
                        key = ("c", d.eng)
                        val = d.cnt
                    if need.get(key, 0) < val:
                        need[key] = val
                if o.dma and o.prevdma is not None:
                    key = ("d", e, o.dslot)
                    need[key] = max(need.get(key, 0), o.prevdma.dcnt)
                for key, val in need.items():
                    if waited.get(key, 0) >= val:
                        continue
                    waited[key] = val
                    sem = self.sem[key[1]] if key[0] == "c" else self.dsem[key[1]][key[2]]
                    eng.wait_ge(sem, val)
                ins = o.fn(eng)
                if o.dma:
                    ins.then_inc(self.dsem[e][o.dslot], 16)
                elif o.signal:
                    ins.then_inc(self.sem[e], 1)
        for e, q in self.dmaops.items():
            eng = self.eobj[e]
            for o in q[-NDMASEM:]:
                eng.wait_ge(self.dsem[e][o.dslot], o.dcnt)


class DryProg:
    def fence(self):
        pass

    def op(self, *a, **k):
        return None

    pe = act = dve = dma = op


def V(t, p0, npart, off, dims):
    F = 1
    for s in t.shape[1:]:
        F *= s
    return bass.AP(t, p0 * F + off, [[F, npart]] + [list(d) for d in dims])


def R(name, c, a, n):
    return [(name, c, b) for b in range(a // 32, (a + n - 1) // 32 + 1)]


def groups(lo, hi, step=512):
    tot = hi - lo
    k = -(-tot // step)
    w = -(-tot // k)
    w = -(-w // 8) * 8
    out = []
    a = lo
    while a < hi:
        n = min(w, hi - a)
        out.append((a, n))
        a += n
    assert len(out) == k and all(n <= step for _, n in out), out
    return out


def pid_w1(l, i):
    return 22 * l + i


def pid_w2(l, i):
    return 22 * l + 4 + i


def pid_mlp(l, e, which):
    base = 22 * l + 6 if l < 2 else 45 + 20 * (l - 2) + 4
    return base + 2 * e + which


PID_KV = 44


def pid_wq(l, i):
    return 45 + 20 * (l - 2) + i


def pid_wo(l, i):
    return 45 + 20 * (l - 2) + 2 + i


NPIECE = 85


class Ring:
    def __init__(self, P, Wt, wst, sched):
        self.P, self.Wt, self.wst = P, Wt, wst
        self.sched = sched
        self.dry = sched is None
        self.log = []
        self.nacq = 0
        self.nload = 0
        self.released = set()

    def acquire(self, pid):
        seq = self.nacq
        self.nacq += 1
        if self.dry:
            self.log.append(pid)
            return seq, seq % NSLOT
        assert self.sched[seq] == pid, (seq, pid, self.sched[seq])
        self._pump()
        assert self.nload > seq, "piece %d not loadable (slot not released)" % seq
        return seq, seq % NSLOT

    def release(self, seq):
        if self.dry:
            return
        self.released.add(seq)
        self._pump()

    def _pump(self):
        while self.nload < len(self.sched) and (self.nload < NSLOT or (self.nload - NSLOT) in self.released):
            s = self.nload % NSLOT
            pid = self.sched[self.nload]
            self.P.dma("pool", V(self.Wt, 0, 128, s * SLOT, [[1, SLOT]]), self.wst[pid], writes=[("w", s)], nofence=True)
            self.nload += 1


def build_program(debug=False):
    nc = bass.Bass("TRN2", target_bir_lowering=False)
    dt = nc.dram_tensor
    xin = dt("xin", [TP, D], F32, kind="ExternalInput")
    xs = dt("xs", [NS, D], F32, kind="ExternalInput")
    sconv = dt("sconv", [2, NS, 30, D], F32, kind="ExternalInput")
    swk = dt("swk", [NS, 128, 256], F32, kind="ExternalInput")
    swv = dt("swv", [NS, 128, 256], F32, kind="ExternalInput")
    hm_d = dt("hm", [128, 2], F32, kind="ExternalInput")
    vecs_d = dt("vecs", [128, NV], F32, kind="ExternalInput")
    gtm_d = dt("gtm", [NS, 3 * 64], F32, kind="ExternalInput")
    cdw = dt("cdw", [2, 31, D], F32, kind="ExternalInput")
    wst = dt("wst", [NPIECE, 128, SLOT], F32, kind="ExternalInput")
    ident_d = dt("ident", [128, 128], F32, kind="ExternalInput")
    bd64_d = dt("bd64", [128, 128], F32, kind="ExternalInput")
    sel4_d = dt("sel4", [120, 4], F32, kind="ExternalInput")
    i16_d = dt("i16", [16, 16], F32, kind="ExternalInput")
    emask_d = dt("emask", [128, 4096], F32, kind="ExternalInput")
    swp_d = dt("swp", [128, 128], F32, kind="ExternalInput")
    ebs_d = dt("ebs", [128, 16], F32, kind="ExternalInput")

    y_o = dt("y", [NO + NS, D], F32, kind="ExternalOutput")
    ncp_o = dt("ncp", [2, 30, D], F32, kind="ExternalOutput")
    ncs_o = dt("ncs", [2, NS, 30, D], F32, kind="ExternalOutput")
    nkp_o = dt("nkp", [128, 256], F32, kind="ExternalOutput")
    nvp_o = dt("nvp", [128, 256], F32, kind="ExternalOutput")
    nks_o = dt("nks", [NS, 128, 256], F32, kind="ExternalOutput")
    nvs_o = dt("nvs", [NS, 128, 256], F32, kind="ExternalOutput")

    with ExitStack() as st:
        def sb(name, shape, dtp=F32):
            return st.enter_context(nc.sbuf_tensor(name, shape, dtp))

        xT = sb("xT", [128, NCH * T])
        AT = sb("AT", [128, NCH * T], BF16)
        Wt = sb("Wt", [128, NSLOT * SLOT], BF16)
        hid = sb("hid", [128, 5 * 512], BF16)
        ident32 = sb("ident32", [128, 128])
        ones32 = sb("ones32", [128, 128])
        bd64 = sb("bd64s", [128, 128])
        identb = sb("identb", [128, 128], BF16)
        onesb = sb("onesb", [128, 128], BF16)
        vecs = sb("vecss", [128, NV])
        hm = sb("hms", [128, 2])
        esink = sb("esink", [128, 32])
        qng8 = sb("qng8", [128, 2])
        i16 = sb("i16s", [16, 16])
        sel4 = sb("sel4s", [120, 4])
        gtm = sb("gtms", [16, 192])
        ebs = sb("ebss", [128, 16])
        u32p = sb("u32p", [128, 8 * 30])
        u32s = sb("u32s", [128, 8 * 16])
        K32 = sb("K32", [128, 256])
        kvs = sb("kvs", [16, 512])
        knew = sb("knew", [16, 256])
        VnT = sb("VnT", [128, 32])
        bd64b = sb("bd64b", [128, 128], BF16)
        swp = sb("swps", [128, 128])
        dnt = sb("dnt", [128, 8])
        scs = sb("scs", [128, 128])
        rq = sb("rq", [16, 16])
        sn = sb("sn", [16, 16])
        rk = sb("rk", [16, 4])
        ssb = sb("ssb", [128, 16])
        pbm = sb("pbm", [128, 16])
        NU = 24448 + 1024
        U = sb("U", [128, NU], BF16)
        pb = [st.enter_context(nc.psum_tensor("pb%d" % i, [128, 512], F32)) for i in range(8)]

        class Carve:
            def __init__(self, t, lo, hi):
                self.t, self.cur, self.hi = t, lo, hi

            def f32(self, nel):
                off = self.cur
                self.cur += 2 * nel
                assert self.cur <= self.hi, (self.cur, self.hi)
                return off

            def b16(self, nel):
                off = self.cur
                self.cur += nel
                assert self.cur <= self.hi, (self.cur, self.hi)
                return off

        cu = Carve(U, 0, NU)
        o_tmp = [cu.f32(512) for _ in range(4)]
        o_g8 = cu.b16(8 * 512)
        ubase = cu.cur
        ca = Carve(U, ubase, NU)
        o_call = ca.f32(8 * 512)
        o_dg = [ca.b16(31 * 128) for _ in range(2)]
        o_dwrep = o_g8
        cb = Carve(U, ubase, NU)
        o_E = cb.b16(4096)
        o_KT = cb.b16(4 * KVW)
        o_Vtm = cb.b16(NKB * 260)
        cat = Carve(AT, 0, NCH * T)
        o_qg = cat.b16(8 * 512)
        o_og = cat.b16(8 * 512)
        abase = cat.cur
        o_P0 = [cat.b16(1024) for _ in range(2)]
        o_PT = [cat.b16(1024) for _ in range(3)]
        o_ogt = [cat.b16(1024) for _ in range(2)]
        cs_ = Carve(AT, abase, NCH * T)
        o_prod = cs_.f32(1024)
        o_KsT = [cs_.b16(256) for _ in range(2)]
        o_QsT = cs_.b16(128)
        o_ssa = cs_.f32(256)
        o_pba = cs_.f32(256)
        o_qtm = cs_.f32(1024)
        o_kst = [cs_.f32(256) for _ in range(2)]
        o_vst = [cs_.f32(256) for _ in range(2)]
        o_dD = cs_.f32(256)
        o_pnb = cs_.f32(256)
        o_on = cs_.f32(256)
        o_dt = cs_.f32(256)

        def UF(off, p0, npart, dims, extra=0):
            d2 = [[2 * s, c] for s, c in dims[:-1]] + [[1, 2 * dims[-1][1]]]
            assert dims[-1][0] == 1
            return V(U, p0, npart, off + 2 * extra, d2).bitcast(F32)

        def AF32(off, p0, npart, dims, extra=0):
            d2 = [[2 * s, c] for s, c in dims[:-1]] + [[1, 2 * dims[-1][1]]]
            assert dims[-1][0] == 1
            return V(AT, p0, npart, off + 2 * extra, d2).bitcast(F32)

        def tmp(i, n, p0=0, npart=128):
            return UF(o_tmp[i], p0, npart, [[1, n]])

        def tmpb(i, n):
            return V(U, 0, 128, o_tmp[i], [[1, n]])

        def xTv(c, a, n):
            return V(xT, 0, 128, c * T + a, [[1, n]])

        def ATv(c, a, n):
            return V(AT, 0, 128, c * T + a, [[1, n]])

        def g8(c, n, a=0):
            return V(U, 0, 128, o_g8 + c * 512 + a, [[1, n]])

        def call(c, a, n):
            return UF(o_call, 0, 128, [[1, n]], extra=c * 512 + a)

        def Wv(s, off, n):
            return V(Wt, 0, 128, s * SLOT + off, [[1, n]])

        def vcol(name, i):
            return V(vecs, 0, 128, VC[name] + i, [[1, 1]])

        def bank(i, n, a=0, p0=0, npart=128):
            return V(pb[i], p0, npart, a, [[1, n]])

        def emit_all(P, W):
            cnt = {"stat": 0, "hb": 0, "yb": 0, "ok": 0}

            def MM(out, lhsT, rhs, start, stop, reads, writes, skip=False):
                P.op("pe", lambda e: e.matmul(out, lhsT=lhsT, rhs=rhs, start=start, stop=stop, skip_group_check=skip),
                     reads, writes)

            def TR(out, in_, idn, reads, writes):
                P.op("pe", lambda e: e.transpose(out, in_, idn), reads, writes)

            def ACTV(out, in_, func, reads, writes, bias=None, scale=1.0):
                if bias is None:
                    P.op("act", lambda e: e.activation(out=out, in_=in_, func=func, scale=scale), reads, writes)
                else:
                    P.op("act", lambda e: e.activation(out=out, in_=in_, func=func, bias=bias, scale=scale), reads, writes)

            def CP(eng, out, in_, reads, writes):
                if eng == "act":
                    P.op("act", lambda e: e.copy(out=out, in_=in_), reads, writes)
                else:
                    P.op("dve", lambda e: e.tensor_copy(out=out, in_=in_), reads, writes)

            def TT(out, in0, in1, op, reads, writes):
                P.op("dve", lambda e: e.tensor_tensor(out=out, in0=in0, in1=in1, op=op), reads, writes)

            def STT(out, in0, scalar, in1, op0, op1, reads, writes):
                P.op("dve", lambda e: e.scalar_tensor_tensor(out=out, in0=in0, scalar=scalar, in1=in1, op0=op0, op1=op1),
                     reads, writes)

            def TS(out, in0, scalar1, op0, reads, writes):
                P.op("dve", lambda e: e.tensor_scalar(out=out, in0=in0, scalar1=scalar1, scalar2=None, op0=op0), reads, writes)

            def RCP(out, in_, reads, writes):
                P.op("dve", lambda e: e.reciprocal(out=out, in_=in_), reads, writes)

            def RED(out, in_, reads, writes):
                P.op("dve", lambda e: e.tensor_reduce(out=out, in_=in_, axis=AX.X, op=ALU.add), reads, writes)

            def BK(i):
                return ("bank", i)

            def RSTD(out, in_, scale, eps, reads, writes):
                ACTV(out, in_, AF.Ln, reads, writes, bias=eps, scale=scale)
                ACTV(out, out, AF.Exp, [], writes, scale=-0.5)

            def POOL_TT(out, in0, in1, op, reads, writes):
                P.op("pool", lambda e: e.tensor_tensor(out=out, in0=in0, in1=in1, op=op), reads, writes)

            def POOL_TS(out, in0, scalar1, op0, reads, writes):
                P.op("pool", lambda e: e.tensor_scalar(out=out, in0=in0, scalar1=scalar1, scalar2=None, op0=op0), reads, writes)

            P.dma("sp", ident32[:], ident_d[:], writes=["ident32"])
            P.dma("sp", bd64[:], bd64_d[:], writes=["bd64"])
            P.dma("sp", vecs[:], vecs_d[:], writes=["vecs"])
            P.dma("sp", hm[:], hm_d[:], writes=["hm"])
            P.dma("sp", i16[:], i16_d[:], writes=["i16"])
            P.dma("sp", sel4[:], sel4_d[:], writes=["sel4"])
            P.dma("sp", gtm[:], gtm_d[:], writes=["gtm"])
            P.dma("sp", ebs[:], ebs_d[:], writes=["ebs"])
            P.dma("sp", swp[:], swp_d[:], writes=["swp"])
            P.op("dve", lambda e: e.memset(ones32[:], 1.0), (), ["ones32"])
            P.op("dve", lambda e: e.memset(onesb[:], 1.0), (), ["onesb"])
            CP("dve", identb[:], ident32[:], ["ident32"], ["identb"])
            CP("dve", bd64b[:], bd64[:], ["bd64"], ["bd64b"])
            ACTV(esink[:], V(vecs, 0, 128, VC["sink"], [[1, 32]]), AF.Exp, ["vecs"], ["esink"])
            TS(qng8[:], V(vecs, 0, 128, VC["qng"], [[1, 2]]), 0.125, ALU.mult, ["vecs"], ["qng8"])

            def load_tok(src_ap, nr, a, k):
                stg = UF(o_call, 0, nr, [[1, 1024]], extra=(k % 4) * 1024)
                P.dma("sp", stg, src_ap, writes=[("io", k % 4)])
                for half in range(2):
                    bi = (2 * k + half) % 4
                    for cc in range(4):
                        c = 4 * half + cc
                        TR(V(pb[bi], 0, 128, cc * 128, [[1, nr]]),
                           UF(o_call, 0, nr, [[1, 128]], extra=(k % 4) * 1024 + c * 128),
                           V(ident32, 0, nr, 0, [[1, nr]]), [("io", k % 4), "ident32"], [BK(bi)])
                    wr = []
                    for cc in range(4):
                        wr += R("x", 4 * half + cc, a, nr)
                    CP("act" if (2 * k + half) % 2 == 0 else "dve",
                       V(xT, 0, 128, 4 * half * T + a, [[T, 4], [1, nr]]), V(pb[bi], 0, 128, 0, [[128, 4], [1, nr]]),
                       [], wr + [BK(bi)])

            xl = {"k": 0, "r0": 0, "done": False}

            def ensure_x(upto):
                while xl["r0"] < min(upto, TP):
                    nr = min(128, TP - xl["r0"])
                    load_tok(xin[xl["r0"]:xl["r0"] + nr, :], nr, xl["r0"], xl["k"])
                    xl["k"] += 1
                    xl["r0"] += nr
                if upto > TP and not xl["done"]:
                    load_tok(xs[:, :], NS, TP, xl["k"])
                    xl["k"] += 1
                    xl["done"] = True
                    for a_ in range(2):
                        P.dma("sp", ncs_o[a_, :, 0:29, :], sconv[a_, :, 1:30, :])
                    P.dma("sp", nks_o[:, 0:127, :], swk[:, 1:128, :])
                    P.dma("sp", nvs_o[:, 0:127, :], swv[:, 1:128, :])

            def norm_stats(a, n, slot):
                bi = 6 + cnt["stat"] % 2
                cnt["stat"] += 1
                for c in range(NCH):
                    ACTV(tmpb(c % 2, n), xTv(c, a, n), AF.Square, R("x", c, a, n), [("tmp", c % 2)])
                    MM(bank(bi, n), onesb[:], tmpb(c % 2, n), c == 0, c == NCH - 1, [("tmp", c % 2), "onesb"], [BK(bi)])
                RSTD(tmp(2 + slot, n), bank(bi, n), 1.0 / D, 1e-6, [], [("tmp", 2 + slot), BK(bi)])

            def norm_apply(gname, gi0, a, n, dst, dres, slot):
                for c in range(NCH):
                    STT(dst(c), xTv(c, a, n), vcol(gname, gi0 + c), tmp(2 + slot, n), ALU.mult, ALU.mult,
                        R("x", c, a, n) + [("tmp", 2 + slot), "vecs"], dres(c))

            def norm_group(gname, gi0, a, n, dst, dres):
                slot = cnt["stat"] % 2
                norm_stats(a, n, slot)
                norm_apply(gname, gi0, a, n, dst, dres, slot)

            def out_tok(dst_ap, nr, a, k):
                for half in range(2):
                    bi = 6 + half
                    for cc in range(4):
                        c = 4 * half + cc
                        TR(V(pb[bi], 0, nr, cc * 128, [[1, 128]]), xTv(c, a, nr), ident32[:],
                           R("x", c, a, nr) + ["ident32"], [BK(bi)])
                    CP("act" if half == 0 else "dve",
                       UF(o_g8, 0, nr, [[1, 512]], extra=(k % 2) * 1024 + half * 512), V(pb[bi], 0, nr, 0, [[1, 512]]),
                       [], [("oio", k % 2, half), BK(bi)])
                P.dma("sp", dst_ap, UF(o_g8, 0, nr, [[1, 1024]], extra=(k % 2) * 1024),
                      reads=[("oio", k % 2, 0), ("oio", k % 2, 1)])

            def emit_out(a, n):
                c = a
                while c < a + n:
                    if c >= TP:
                        out_tok(y_o[NO:NO + NS, :], NS, TP, cnt["ok"])
                        cnt["ok"] += 1
                        c += NS
                    else:
                        nr = min(128, min(a + n, TP) - c)
                        r = c - H
                        out_tok(y_o[r:r + nr, :], nr, c, cnt["ok"])
                        cnt["ok"] += 1
                        c += nr

            def mlp_norm(l, a, n):
                norm_group("nmlp", 8 * l, a, n, lambda c, a=a, n=n: ATv(c, a, n), lambda c, a=a, n=n: R("at", c, a, n))

            def mlp(l, grps, do_norm=True, after_unit=None):
                nrm = {"n": 0 if do_norm else len(grps)}

                def need_norm(g):
                    while nrm["n"] <= min(g, len(grps) - 1):
                        mlp_norm(l, grps[nrm["n"]][0], grps[nrm["n"]][1])
                        nrm["n"] += 1

                need_norm(1)
                units = [(e8, gi) for e8 in range(8) for gi in range(len(grps))]
                acq = {}

                def ensure(e8):
                    if e8 not in acq:
                        q1, s1 = W.acquire(pid_mlp(l, e8, 0))
                        q2, s2 = W.acquire(pid_mlp(l, e8, 1))
                        acq[e8] = (q1, s1, q2, s2)

                def hslot(ui, jb):
                    return (4 * ui + jb) % 5

                def w1_block(ui, jb):
                    e8, gi = units[ui]
                    ensure(e8)
                    a, n = grps[gi]
                    s1 = acq[e8][1]
                    hs = hslot(ui, jb)
                    bi = cnt["hb"] % 3
                    cnt["hb"] += 1
                    for kc in range(NCH):
                        MM(bank(bi, n), Wv(s1, kc * 512 + jb * 128, 128), ATv(kc, a, n), kc == 0, kc == NCH - 1,
                           [("w", s1)] + R("at", kc, a, n), [BK(bi)])
                    ti = cnt["hb"] % 2
                    ACTV(tmp(ti, n), bank(bi, n), AF.Relu, [], [("tmp", ti), BK(bi)])
                    ACTV(V(hid, 0, 128, hs * 512, [[1, n]]), tmp(ti, n), AF.Square, [("tmp", ti)], [("hid", hs)])

                w1_block(0, 0)
                for ui in range(len(units)):
                    e8, gi = units[ui]
                    a, n = grps[gi]
                    need_norm(gi + 2)
                    for jb in range(1, 4):
                        w1_block(ui, jb)
                    if ui + 1 < len(units):
                        w1_block(ui + 1, 0)
                    if gi == len(grps) - 1:
                        W.release(acq[e8][0])
                    s2 = acq[e8][3]
                    for oc in range(NCH):
                        bi = 3 + cnt["yb"] % 3
                        cnt["yb"] += 1
                        for jb in range(4):
                            hs = hslot(ui, jb)
                            MM(bank(bi, n), Wv(s2, jb * 1024 + oc * 128, 128), V(hid, 0, 128, hs * 512, [[1, n]]),
                               jb == 0, jb == 3, [("w", s2), ("hid", hs)], [BK(bi)])
                        TT(xTv(oc, a, n), bank(bi, n), xTv(oc, a, n), ALU.add, [], R("x", oc, a, n) + [BK(bi)])
                    if l == 3 and e8 == 7:
                        emit_out(a, n)
                    if after_unit is not None:
                        after_unit(e8, gi)
                    if gi == len(grps) - 1:
                        W.release(acq[e8][2])

            sgm = V(hid, 0, 128, 0, [[1, 2048]]).bitcast(F32)

            def sgv(a0, n):
                return sgm[:, a0:a0 + n]

            for l in range(2):
                lo_i = 0 if l == 0 else 32
                lo_ii = 32 if l == 0 else 64
                g_i = groups(lo_i, T)
                ensure_x(g_i[0][0] + g_i[0][1])
                norm_stats(g_i[0][0], g_i[0][1], 0)
                for gidx, (a, n) in enumerate(g_i):
                    norm_apply("nmix", 8 * l, a, n, lambda c, n=n: g8(c, n), lambda c: [("g8", c)], gidx % 2)
                    if gidx + 1 < len(g_i):
                        ensure_x(g_i[gidx + 1][0] + g_i[gidx + 1][1])
                        norm_stats(g_i[gidx + 1][0], g_i[gidx + 1][1], (gidx + 1) % 2)
                    for i in range(4):
                        q, s = W.acquire(pid_w1(l, i))
                        for cc in range(2):
                            uc = 2 * i + cc
                            bA, bB = (0, 1) if uc % 2 == 0 else (2, 3)
                            for kc in range(NCH):
                                MM(bank(bA, n), Wv(s, kc * 512 + cc * 128, 128), g8(kc, n), kc == 0, kc == NCH - 1,
                                   [("w", s), ("g8", kc)], [BK(bA)])
                            for kc in range(NCH):
                                MM(bank(bB, n), Wv(s, kc * 512 + 256 + cc * 128, 128), g8(kc, n), kc == 0, kc == NCH - 1,
                                   [("w", s), ("g8", kc)], [BK(bB)])
                            ACTV(sgv(0, n), bank(bB, n), AF.Sigmoid, ["vecs"], ["sgm", BK(bB)],
                                 bias=vcol("b1", 16 * l + 8 + uc))
                            STT(ATv(uc, a, n), bank(bA, n), vcol("b1", 16 * l + uc), sgv(0, n), ALU.add, ALU.mult,
                                ["sgm", "vecs"], R("at", uc, a, n) + [BK(bA)])
                            if a <= TP - 30 and a + n >= T:
                                o1 = TP - 30 - a
                                STT(V(u32p, 0, 128, uc * 30, [[1, 30]]), bank(bA, 30, o1), vcol("b1", 16 * l + uc),
                                    sgv(o1, 30), ALU.add, ALU.mult, ["sgm", "vecs"], [("u32p", uc), BK(bA)])
                                o2 = TP - a
                                STT(V(u32s, 0, 128, uc * 16, [[1, 16]]), bank(bA, 16, o2), vcol("b1", 16 * l + uc),
                                    sgv(o2, 16), ALU.add, ALU.mult, ["sgm", "vecs"], [("u32s", uc), BK(bA)])
                            if a < H:
                                m = min(H, a + n) - a
                                TS(ATv(uc, a, m), ATv(uc, a, m), hm[:, 0:1], ALU.mult, ["hm"], R("at", uc, a, m))
                        W.release(q)
                for (src, nr, res, eoff, dst) in ((u32p, 30, "u32p", 0, ncp_o[l, :, :]), (u32s, 16, "u32s", 1024, ncs_o[l, :, 29, :])):
                    for half in range(2):
                        for cc in range(4):
                            c = 4 * half + cc
                            TR(V(pb[4 + half], 0, nr, cc * 128, [[1, 128]]), V(src, 0, 128, c * nr, [[1, nr]]), ident32[:],
                               [(res, c), "ident32"], [BK(4 + half)])
                        CP("act", UF(o_call, 0, nr, [[1, 512]], extra=eoff + half * 512), V(pb[4 + half], 0, nr, 0, [[1, 512]]),
                           [], [("cio", res, half), BK(4 + half)] + [("io", i_) for i_ in range(4)])
                    P.dma("sp", dst, UF(o_call, 0, nr, [[1, 1024]], extra=eoff), reads=[("cio", res, 0), ("cio", res, 1)])
                P.fence()
                g8all = [("g8", c) for c in range(NCH)]
                callall = [("call", c) for c in range(NCH)]
                for r in range(4):
                    P.dma("sp", UF(o_dwrep, 30 * r, 30, [[1, 1024]]), cdw[l, 0:30, :], writes=[("dwrep", r)])
                for t4 in range(4):
                    P.dma("sp", UF(o_call, 0, 120, [[1, 1024]], extra=t4 * 1024),
                          sconv[l, 4 * t4:4 * t4 + 4, :, :].rearrange("b k d -> (b k) d"), writes=[("st", t4)])
                for t4 in range(4):
                    TT(UF(o_call, 0, 120, [[1, 1024]], extra=t4 * 1024), UF(o_call, 0, 120, [[1, 1024]], extra=t4 * 1024),
                       UF(o_dwrep, 0, 120, [[1, 1024]]), ALU.mult, [("dwrep", r) for r in range(4)] + g8all, [("st", t4)])
                    for uc in range(NCH):
                        MM(bank(6, 4, uc * 16 + t4 * 4), UF(o_call, 0, 120, [[1, 128]], extra=t4 * 1024 + uc * 128), sel4[:],
                           uc == 0 and t4 == 0, True, [("st", t4), "sel4"] + callall, [BK(6)], skip=True)
                for uc in range(NCH):
                    ACTV(V(scs, 0, 128, uc * 16, [[1, 16]]), bank(6, 16, uc * 16), AF.Identity, ["vecs"], [("scs", uc), BK(6)],
                         bias=vcol("dwb", 8 * l + uc))
                g_ii = groups(lo_ii, T)

                def split(a, n):
                    ns = NS if a + n == T else 0
                    return n - ns, ns

                def conv_build(uc):
                    di = uc % 2
                    TT(V(U, 0, 128, o_dg[di], [[128, 31], [1, 128]]), V(identb, 0, 128, 0, [[0, 31], [1, 128]]),
                       V(vecs, 0, 128, VC["dwT"] + l * 248 + uc, [[8, 31], [0, 128]]), ALU.mult, ["identb", "vecs"], [("dg", di)])

                def conv_mm(a, n, uc, built=False):
                    np_, ns = split(a, n)
                    di = uc % 2
                    if not built:
                        conv_build(uc)
                    bi = uc % 2
                    for k in range(31):
                        MM(bank(bi, np_), V(U, 0, 128, o_dg[di] + k * 128, [[1, 128]]), ATv(uc, a - 30 + k, np_), k == 0, k == 30,
                           [("dg", di)] + R("at", uc, a - 30 + k, np_), [BK(bi)])
                    if ns:
                        MM(bank(6, 16, uc * 16), V(U, 0, 128, o_dg[di] + 30 * 128, [[1, 128]]), ATv(uc, TP, NS), uc == 0, True,
                           [("dg", di)] + R("at", uc, TP, NS), [BK(6)], skip=True)

                hidall = [("hid", i) for i in range(5)]

                def cl(par, uc, a0, n):
                    if uc == 0 and par == 1:
                        return sgv(a0, n)
                    return call(uc, a0, n)

                def clr(par, uc):
                    return hidall if (uc == 0 and par == 1) else [("call", uc)]

                def conv_evac(a, n, uc, par):
                    np_, ns = split(a, n)
                    bi = uc % 2
                    ACTV(cl(par, uc, 0, np_), bank(bi, np_), AF.Identity, ["vecs"], clr(par, uc) + [BK(bi)],
                         bias=vcol("dwb", 8 * l + uc))
                    if ns:
                        TT(cl(par, uc, np_, NS), bank(6, 16, uc * 16), V(scs, 0, 128, uc * 16, [[1, 16]]), ALU.add,
                           [("scs", uc)], clr(par, uc) + [BK(6)])

                def conv_sum(a, n, uc, par):
                    MM(bank(4, n), ones32[:], cl(par, uc, 0, n), uc == 0, uc == 7, clr(par, uc) + ["ones32"], [BK(4)])

                def conv_rest(a, n, par, first_uc, evac_done=False, nbuilt=0):
                    pending = []
                    for uc in range(first_uc):
                        if not evac_done:
                            conv_evac(a, n, uc, par)
                        pending.append(uc)
                    nb = first_uc + nbuilt
                    for uc in range(first_uc, NCH):
                        conv_mm(a, n, uc, built=(uc < nb))
                        nb = max(nb, uc + 1)
                        if nb < NCH and nb <= uc + 1:
                            conv_build(nb)
                            nb += 1
                        for p_ in pending:
                            conv_sum(a, n, p_, par)
                        pending = []
                        conv_evac(a, n, uc, par)
                        pending.append(uc)
                    for p_ in pending:
                        conv_sum(a, n, p_, par)

                def ln_stage(a, n, par, after_silu=None, mid=None):
                    TS(tmp(3, n), bank(4, n), -1.0 / D, ALU.mult, [], [("tmp", 3), BK(4)])
                    if mid is not None:
                        mid()
                    for uc in range(NCH):
                        TT(cl(par, uc, 0, n), cl(par, uc, 0, n), tmp(3, n), ALU.add, [("tmp", 3)], clr(par, uc))
                        ACTV(tmpb(uc % 2, n), cl(par, uc, 0, n), AF.Square, clr(par, uc), [("tmp", uc % 2)])
                        MM(bank(5, n), onesb[:], tmpb(uc % 2, n), uc == 0, uc == 7, [("tmp", uc % 2), "onesb"], [BK(5)])
                    RSTD(tmp(3, n), bank(5, n), 1.0 / D, 1e-5, [], [("tmp", 3), BK(5)])
                    for uc in range(NCH):
                        TT(cl(par, uc, 0, n), cl(par, uc, 0, n), tmp(3, n), ALU.mult, [("tmp", 3)], clr(par, uc))
                        ACTV(g8(uc, n), cl(par, uc, 0, n), AF.Silu, clr(par, uc) + ["vecs"], [("g8", uc)],
                             bias=vcol("lnb", 8 * l + uc), scale=vcol("lng", 8 * l + uc))
                        if after_silu is not None:
                            after_silu(uc)

                def w2_stage(a, n):
                    for i in range(2):
                        q, s = W.acquire(pid_w2(l, i))
                        for cc in range(4):
                            oc = 4 * i + cc
                            bi = 2 + oc % 2
                            for kc in range(NCH):
                                MM(bank(bi, n), Wv(s, kc * 512 + cc * 128, 128), g8(kc, n), kc == 0, kc == NCH - 1,
                                   [("w", s), ("g8", kc)], [BK(bi)])
                            STT(xTv(oc, a, n), bank(bi, n), vcol("b2", 8 * l + oc), xTv(oc, a, n), ALU.add, ALU.add,
                                ["vecs"], R("x", oc, a, n) + [BK(bi)])
                        W.release(q)

                NPRE = 2
                conv_rest(g_ii[0][0], g_ii[0][1], 0, 0)
                prebuilt = False
                for gidx, (a, n) in enumerate(g_ii):
                    par = gidx % 2
                    nxt = g_ii[gidx + 1] if gidx + 1 < len(g_ii) else None
                    if nxt:
                        for uc in range(NPRE):
                            conv_mm(nxt[0], nxt[1], uc, built=prebuilt)
                        conv_evac(nxt[0], nxt[1], 0, 1 - par)
                        ln_stage(a, n, par,
                                 lambda uc, nxt=nxt, par=par: conv_evac(nxt[0], nxt[1], uc, 1 - par) if uc == 1 else None,
                                 mid=lambda: conv_build(NPRE))
                        conv_rest(nxt[0], nxt[1], 1 - par, NPRE, evac_done=True, nbuilt=1)
                    else:
                        ln_stage(a, n, par)
                    w2_stage(a, n)
                    if gidx + 2 < len(g_ii):
                        conv_build(0)
                        conv_build(1)
                        prebuilt = True
                    else:
                        prebuilt = False
                    mlp_norm(l, a, n)
                kvg = groups(64, T)

                def kv_norm_hook(e8, gi):
                    if l == 1 and e8 == 7:
                        a_, n_ = kvg[gi]
                        norm_group("kvn", 0, a_, n_, lambda c, a_=a_, n_=n_: ATv(c, a_, n_), lambda c, a_=a_, n_=n_: R("at", c, a_, n_))

                mlp(l, groups(lo_ii, T), do_norm=False, after_unit=kv_norm_hook)
                P.fence()

            P.dma("pool", V(U, 0, 128, o_E, [[1, 4096]]), emask_d[:], writes=["E"])
            kvg = groups(64, T)
            P.op("dve", lambda e: e.memset(V(U, 0, 128, o_KT, [[1, 4 * KVW]]), 0.0), (), ["KTzero"])
            q, s = W.acquire(PID_KV)
            ki = 0
            for t in range(2):
                for (a, n) in kvg:
                    np_ = min(a + n, TP) - a
                    bi = ki % 2
                    bs = 2 + ki % 2
                    ki += 1
                    for kc in range(NCH):
                        MM(bank(bi, np_), Wv(s, kc * 512 + t * 128, 128), ATv(kc, a, np_), kc == 0, kc == NCH - 1,
                           [("w", s)] + R("at", kc, a, np_), [BK(bi)])
                    ACTV(tmpb(bi, np_), bank(bi, np_), AF.Square, [], [("tmp", bi), BK(bi)])
                    MM(bank(bs, np_), bd64b[:], tmpb(bi, np_), True, True, [("tmp", bi), "bd64b"], [BK(bs)])
                    RSTD(tmp(3, np_), bank(bs, np_), 1.0 / 64, 1e-6, [], [("tmp", 3), BK(bs)])
                    for hf_ in range(2):
                        STT(V(U, hf_ * 64, 64, o_KT + (2 * t + hf_) * KVW + a - 64, [[1, np_]]), V(pb[bi], hf_ * 64, 64, 0, [[1, np_]]),
                            V(vecs, hf_ * 64, 64, VC["kng"], [[1, 1]]), UF(o_tmp[3], hf_ * 64, 64, [[1, np_]]),
                            ALU.mult, ALU.mult, [("tmp", 3), "vecs", "KTzero"], [("KT", t, a, hf_), BK(bi)])
                    if a <= TP - 128 and a + np_ >= TP:
                        o1 = TP - 128 - a
                        STT(V(K32, 0, 128, t * 128, [[1, 128]]), bank(bi, 128, o1), vcol("kng", 0),
                            UF(o_tmp[3], 0, 128, [[1, 128]], extra=o1), ALU.mult, ALU.mult,
                            [("tmp", 3), "vecs"], [("K32", t), BK(bi)])
            P.op("dve", lambda e: e.memset(V(U, 0, 128, o_Vtm + 64, [[65, NKB * 4], [1, 1]]), 1.0), (), ["Vones"])
            for blk in range(NKB):
                a = 64 + blk * 128
                bi = 4 + blk % 2
                for kc in range(NCH):
                    MM(bank(bi, 256), ATv(kc, a, 128), Wv(s, kc * 512 + 256, 256), kc == 0, kc == NCH - 1,
                       [("w", s)] + R("at", kc, a, 128), [BK(bi)])
                CP("act", V(U, 0, 128, o_Vtm + blk * 260, [[65, 4], [1, 64]]), V(pb[bi], 0, 128, 0, [[64, 4], [1, 64]]),
                   [], [("Vtm", blk), BK(bi)])
                if blk == NKB - 1:
                    CP("dve", tmp(0, 256), bank(bi, 256), [], [("tmp", 0), BK(bi)])
                    P.dma("sp", nvp_o[:, :], tmp(0, 256), reads=[("tmp", 0)], writes=[("tmp", 0)])
            TS(V(U, 0, 128, o_Vtm, [[1, 260]]), V(U, 0, 128, o_Vtm, [[1, 260]]), hm[:, 0:1], ALU.mult,
               ["hm"], [("Vtm", 0), "Vones"])
            for kc in range(NCH):
                MM(bank(6, 512, 0, 0, 16), ATv(kc, TP, NS), Wv(s, kc * 512, 512), kc == 0, kc == NCH - 1,
                   [("w", s)] + R("at", kc, TP, NS), [BK(6)])
            CP("act", kvs[:], bank(6, 512, 0, 0, 16), [], ["kvs", BK(6)])
            TT(tmp(1, 256, 0, 16), kvs[:, 0:256], kvs[:, 0:256], ALU.mult, ["kvs"], [("tmp", 1)])
            RED(rk[:], UF(o_tmp[1], 0, 16, [[64, 4], [1, 64]]), [("tmp", 1)], ["rk"])
            RSTD(rk[:], rk[:], 1.0 / 64, 1e-6, [], ["rk"])
            TT(V(knew, 0, 16, 0, [[64, 4], [1, 64]]), V(kvs, 0, 16, 0, [[64, 4], [1, 64]]), V(rk, 0, 16, 0, [[1, 4], [0, 64]]),
               ALU.mult, ["kvs", "rk"], ["knew"])
            TT(V(knew, 0, 16, 0, [[64, 4], [1, 64]]), V(knew, 0, 16, 0, [[64, 4], [1, 64]]), V(gtm, 0, 16, 0, [[0, 4], [1, 64]]),
               ALU.mult, ["gtm"], ["knew"])
            P.dma("sp", nks_o[:, 127, :], knew[:], reads=["knew"])
            P.dma("sp", nvs_o[:, 127, :], kvs[:, 256:512], reads=["kvs"])
            for t in range(2):
                for kc in range(NCH):
                    MM(bank(7, 16, t * 16), Wv(s, kc * 512 + 256 + t * 128, 128), ATv(kc, TP, NS), kc == 0, kc == NCH - 1,
                       [("w", s)] + R("at", kc, TP, NS), [BK(7)], skip=True)
            CP("act", VnT[:], bank(7, 32), [], ["VnT", BK(7)])
            W.release(q)
            for t in range(2):
                TR(bank(0, 128, t * 128), V(K32, 0, 128, t * 128, [[1, 128]]), ident32[:], [("K32", t), "ident32"], [BK(0)])
            CP("dve", tmp(1, 256), bank(0, 256), [], [("tmp", 1), BK(0)])
            P.dma("sp", nkp_o[:, :], tmp(1, 256), reads=[("tmp", 1)], writes=[("tmp", 1)])
            P.fence()

            def ogv(c, a, n):
                return V(AT, 0, 128, o_og + c * 512 + a, [[1, n]])

            def wo_group(l, a, n):
                for i in range(2):
                    q, s = W.acquire(pid_wo(l, i))
                    for cc in range(4):
                        oc = 4 * i + cc
                        bi = oc % 2
                        for kc in range(NCH):
                            MM(bank(bi, n), Wv(s, kc * 512 + cc * 128, 128), ogv(kc, 0, n), kc == 0, kc == NCH - 1,
                               [("w", s), ("og", kc)], [BK(bi)])
                        TT(xTv(oc, a, n), bank(bi, n), xTv(oc, a, n), ALU.add, [], R("x", oc, a, n) + [BK(bi)])
                    W.release(q)

            for l in (2, 3):
                lb = l - 2
                g_b = [(H + 512 * gi, 512) for gi in range(4)]
                norm_stats(g_b[0][0], 512, 0)
                for gi in range(4):
                    a, n = g_b[gi]
                    norm_apply("nmix", 8 * l, a, n, lambda c, n=n: g8(c, n), lambda c: [("g8", c)], gi % 2)
                    if gi + 1 < 4:
                        norm_stats(g_b[gi + 1][0], 512, (gi + 1) % 2)
                    QB = (0, 1, 4, 5)

                    def qfin(qc):
                        ti = qc % 2
                        bi = QB[qc % 4]
                        bs = 2 + qc % 2
                        MM(bank(bs, n), bd64b[:], tmpb(ti, n), True, True, [("tmp", ti), "bd64b"], [BK(bs)])
                        RSTD(sgv(ti * 512, n), bank(bs, n), 1.0 / 64, 1e-6, [], [("qr", ti), BK(bs)])
                        STT(V(AT, 0, 128, o_qg + qc * 512, [[1, n]]), bank(bi, n), qng8[:, lb:lb + 1], sgv(ti * 512, n),
                            ALU.mult, ALU.mult, [("qr", ti), "qng8"], [("qg", qc), BK(bi)])

                    pend = None
                    for i in range(2):
                        q, s = W.acquire(pid_wq(l, i))
                        for cc in range(4):
                            qc = 4 * i + cc
                            bi = QB[qc % 4]
                            for kc in range(NCH):
                                MM(bank(bi, n), Wv(s, kc * 512 + cc * 128, 128), g8(kc, n), kc == 0, kc == NCH - 1,
                                   [("w", s), ("g8", kc)], [BK(bi)])
                            ACTV(tmpb(qc % 2, n), bank(bi, n), AF.Square, [], [("tmp", qc % 2), BK(bi)])
                            if pend is not None:
                                qfin(pend)
                            pend = qc
                        W.release(q)
                    qfin(pend)

                    def stageA(qb, j):
                        pt = (4 * qb + j) % 3
                        Bk = 4 * gi + qb
                        t, hf = j // 2, j % 2
                        r0 = hf * 64
                        cs = (j // 2) * 4
                        pj = j % 2
                        bS = (0, 1) if pj == 0 else (2, 3)
                        for kb in range(2):
                            MM(bank(bS[kb], 512), V(U, 0, 128, o_KT + j * KVW + (Bk + kb) * 128, [[1, 128]]),
                               V(AT, 0, 128, o_qg + cs * 512 + qb * 128, [[512, 4], [1, 128]]), True, True,
                               ["KT"] + [("qg", cs + i) for i in range(4)], [BK(bS[kb])])
                            ACTV(V(AT, 0, 128, o_P0[pj] + kb * 512, [[1, 512]]), bank(bS[kb], 512), AF.Exp,
                                 [], [("P0", pj, kb), BK(bS[kb])])
                        POOL_TT(V(AT, 0, 128, o_PT[pt], [[1, 256]]), V(AT, 0, 128, o_P0[pj], [[1, 256]]),
                                V(U, 0, 128, o_E + j * 1024, [[1, 256]]), ALU.mult,
                                [("P0", pj, 0), "E"], [("PT", pt, 0, "a")])
                        TT(V(AT, 0, 128, o_PT[pt] + 256, [[1, 768]]), V(AT, 0, 128, o_P0[pj] + 256, [[1, 768]]),
                           V(U, 0, 128, o_E + j * 1024 + 256, [[1, 768]]), ALU.mult,
                           [("P0", pj, 0), ("P0", pj, 1), "E"], [("PT", pt, 0), ("PT", pt, 1)])

                    def stageB(qb, j):
                        pt = (4 * qb + j) % 3
                        Bk = 4 * gi + qb
                        hf = j % 2
                        cs = (j // 2) * 4
                        pj = j % 2
                        bO = 4 + pj
                        for i in range(4):
                            for kb in range(2):
                                MM(V(pb[bO], 0, 128, i * 65, [[1, 65]]), V(AT, 0, 128, o_PT[pt] + kb * 512 + i * 128, [[1, 128]]),
                                   V(U, 0, 128, o_Vtm + ((Bk + kb) * 4 + j) * 65, [[1, 65]]), kb == 0, kb == 1,
                                   ["Vtm", "Vones", ("PT", pt, kb), ("PT", pt, 0, "a")], [BK(bO)], skip=True)
                        dj = V(dnt, 0, 128, pj * 4, [[1, 4]])
                        TT(dj, V(pb[bO], 0, 128, 64, [[65, 4]]), V(esink, 0, 128, lb * 16 + 2 * cs + hf, [[2, 4]]), ALU.add,
                           ["esink"], [("dnt", pj), BK(bO)])
                        RCP(dj, dj, [], [("dnt", pj)])
                        og_i = Bk % 2
                        TT(V(AT, 0, 128, o_ogt[og_i] + j * 256, [[64, 4], [1, 64]]), V(pb[bO], 0, 128, 0, [[65, 4], [1, 64]]),
                           V(dnt, 0, 128, pj * 4, [[1, 4], [0, 64]]), ALU.mult,
                           [("dnt", pj)], [("ogt", og_i, j), BK(bO)])

                    def stageC(qb):
                        Bk = 4 * gi + qb
                        og_i = Bk % 2
                        bT = 6 + Bk % 2
                        pTv = V(pb[bT], 0, 128, 0, [[1, 512]]).bitcast(BF16)
                        for c in range(NCH):
                            TR(pTv[:, c * 128:(c + 1) * 128], V(AT, 0, 128, o_ogt[og_i] + c * 128, [[1, 128]]), identb[:],
                               [("ogt", og_i, c // 2), "identb"], [BK(bT)])
                        CP("act", V(AT, 0, 128, o_og + qb * 128, [[512, 4], [1, 128]]), pTv[:, 0:512].rearrange("p (c q) -> p c q", q=128),
                           [], [("og", c) for c in range(4)] + [BK(bT)])
                        CP("dve", V(AT, 0, 128, o_og + 4 * 512 + qb * 128, [[512, 4], [1, 128]]),
                           pTv[:, 512:1024].rearrange("p (c q) -> p c q", q=128),
                           [], [("og", c) for c in range(4, 8)] + [BK(bT)])

                    items = [(qb, j) for qb in range(4) for j in range(4)]
                    LAG = 2
                    for k in range(min(LAG, len(items))):
                        stageA(*items[k])
                    pendC = None
                    for k in range(len(items)):
                        if k + LAG < len(items):
                            stageA(*items[k + LAG])
                        stageB(*items[k])
                        if pendC is not None:
                            stageC(pendC)
                            pendC = None
                        if items[k][1] == 3:
                            pendC = items[k][0]
                    if pendC is not None:
                        stageC(pendC)
                    wo_group(l, a, n)
                P.fence()
                a, n = TP, NS
                norm_group("nmix", 8 * l, a, n, lambda c, n=n: g8(c, n), lambda c: [("g8", c)])
                for i in range(2):
                    q, s = W.acquire(pid_wq(l, i))
                    for kc in range(NCH):
                        MM(bank(i, 512, 0, 0, 16), g8(kc, NS), Wv(s, kc * 512, 512), kc == 0, kc == NCH - 1,
                           [("w", s), ("g8", kc)], [BK(i)])
                    CP("act", AF32(o_qtm, 0, 16, [[1, 512]], extra=i * 512), bank(i, 512, 0, 0, 16), [], ["qtm", BK(i)])
                    W.release(q)
                qt3 = AF32(o_qtm, 0, 16, [[64, 16], [1, 64]])
                pr3 = AF32(o_prod, 0, 16, [[64, 16], [1, 64]])
                TT(AF32(o_prod, 0, 16, [[1, 1024]]), AF32(o_qtm, 0, 16, [[1, 1024]]), AF32(o_qtm, 0, 16, [[1, 1024]]), ALU.mult,
                   ["qtm"], ["prod"])
                RED(sn[:], pr3, ["prod"], ["sn"])
                RSTD(rq[:], sn[:], 1.0 / 64, 1e-6, ["sn"], ["rq"])
                TT(qt3, qt3, V(rq, 0, 16, 0, [[1, 16], [0, 64]]), ALU.mult, ["rq"], ["qtm"])
                STT(qt3, qt3, 0.125, V(gtm, 0, 16, 64 * (1 + lb), [[0, 16], [1, 64]]), ALU.mult, ALU.mult, ["gtm"], ["qtm"])
                for j in range(4):
                    hf, cs = j % 2, (j // 2) * 4
                    p0 = 2 * cs + hf
                    TT(AF32(o_prod, 0, 16, [[128, 4], [1, 64]], extra=p0 * 64), AF32(o_qtm, 0, 16, [[128, 4], [1, 64]], extra=p0 * 64),
                       V(knew, 0, 16, j * 64, [[0, 4], [1, 64]]), ALU.mult, ["qtm", "knew"], ["prod"])
                RED(sn[:], pr3, ["prod"], ["sn"])
                ACTV(sn[:], sn[:], AF.Exp, [], ["sn"])
                for c in range(NCH):
                    TR(V(pb[2], 0, 128, c * 16, [[1, 16]]), AF32(o_qtm, 0, 16, [[1, 128]], extra=c * 128), V(ident32, 0, 16, 0, [[1, 16]]),
                       ["qtm", "ident32"], [BK(2)])
                CP("act", V(AT, 0, 128, o_QsT, [[1, 128]]), bank(2, 128), [], ["QsT", BK(2)])
                for b in range(NS):
                    k3 = b % 2
                    P.dma("sp", AF32(o_kst[k3], 0, 128, [[1, 256]]), swk[b, :, :], writes=[("kst", k3)])
                    bt = 3 if b % 2 == 0 else 7
                    for t in range(2):
                        TR(bank(bt, 128, t * 128), AF32(o_kst[k3], 0, 128, [[1, 128]], extra=t * 128), ident32[:],
                           [("kst", k3), "ident32"], [BK(bt)])
                    CP("act" if b % 2 == 0 else "dve", V(AT, 0, 128, o_KsT[b % 2], [[1, 256]]), bank(bt, 256), [],
                       [("KsT", b % 2), BK(bt)])
                    for j in range(4):
                        t, hf = j // 2, j % 2
                        r0 = hf * 64
                        cs = (j // 2) * 4
                        MM(V(pb[6], 0, 128, b * 16 + 2 * cs + hf, [[2, 4]]), V(AT, r0, 64, o_KsT[b % 2] + t * 128, [[1, 128]]),
                           V(AT, r0, 64, o_QsT + cs * 16 + b, [[16, 4]]), b == 0 and j == 0, True,
                           [("KsT", b % 2), "QsT"], [BK(6)], skip=True)
                TT(AF32(o_ssa, 0, 128, [[16, 16], [1, 16]]), V(pb[6], 0, 128, 0, [[16, 16], [1, 16]]), V(ebs, 0, 128, 0, [[0, 16], [1, 16]]),
                   ALU.add, ["ebs"], ["ssa", BK(6)])
                ACTV(AF32(o_pba, 0, 128, [[1, 256]]), AF32(o_ssa, 0, 128, [[1, 256]]), AF.Exp, ["ssa"], ["pba"])
                for b in range(NS):
                    k3 = b % 2
                    P.dma("sp", AF32(o_vst[k3], 0, 128, [[1, 256]]), swv[b, :, :], writes=[("vst", k3)])
                    for t in range(2):
                        MM(V(pb[4], 0, 128, t * 128 + b, [[16, 8]]), AF32(o_vst[k3], 0, 128, [[1, 128]], extra=t * 128),
                           AF32(o_pba, 0, 128, [[1, 8]], extra=b * 16 + t * 8), b == 0 and t == 0, True,
                           [("vst", k3), "pba"], [BK(4)], skip=True)
                MM(bank(5, 256), ones32[:], AF32(o_pba, 0, 128, [[1, 256]]), True, True, ["pba", "ones32"], [BK(5)])
                TT(AF32(o_dD, 0, 16, [[16, 16], [1, 16]]), V(sn, 0, 16, 0, [[1, 16], [0, 16]]), V(i16, 0, 16, 0, [[0, 16], [1, 16]]),
                   ALU.mult, ["sn", "i16"], ["dD"])
                MM(bank(6, 256), ones32[0:16, :], AF32(o_dD, 0, 16, [[1, 256]]), True, True, ["dD", "ones32"], [BK(6)])
                CP("act", AF32(o_pnb, 0, 128, [[1, 256]]), bank(6, 256), [], ["pnb", BK(6)])
                TT(AF32(o_on, 0, 128, [[128, 2], [16, 8], [1, 16]]), AF32(o_pnb, 0, 128, [[128, 2], [16, 8], [1, 16]]),
                   V(VnT, 0, 128, 0, [[16, 2], [0, 8], [1, 16]]), ALU.mult, ["pnb", "VnT"], ["on"])
                TT(AF32(o_on, 0, 128, [[1, 256]]), bank(4, 256), AF32(o_on, 0, 128, [[1, 256]]), ALU.add, [], ["on", BK(4)])
                TT(AF32(o_dt, 0, 128, [[16, 16], [1, 16]]), V(pb[5], 0, 128, 0, [[1, 16], [16, 16]]),
                   AF32(o_pnb, 0, 128, [[16, 16], [1, 16]]), ALU.add, ["pnb"], ["dt", BK(5)])
                TT(AF32(o_dt, 0, 128, [[16, 16], [1, 16]]), AF32(o_dt, 0, 128, [[16, 16], [1, 16]]),
                   V(esink, 0, 128, lb * 16, [[1, 16], [0, 16]]), ALU.add, ["esink"], ["dt"])
                RCP(AF32(o_dt, 0, 128, [[1, 256]]), AF32(o_dt, 0, 128, [[1, 256]]), [], ["dt"])
                TT(AF32(o_on, 0, 128, [[1, 256]]), AF32(o_on, 0, 128, [[1, 256]]), AF32(o_dt, 0, 128, [[1, 256]]), ALU.mult,
                   ["dt"], ["on"])
                MM(bank(7, 256), swp[:], AF32(o_on, 0, 128, [[1, 256]]), True, True, ["on", "swp"], [BK(7)])
                for h in range(16):
                    p = PERM.index(h)
                    sh, dh = p % 2, h % 2
                    dst = V(AT, dh * 64, 64, o_og + (h // 2) * 512, [[1, 16]])
                    if sh == dh:
                        CP("dve", dst, AF32(o_on, dh * 64, 64, [[1, 16]], extra=p * 16), ["on"], [("og", h // 2)])
                    else:
                        CP("dve", dst, V(pb[7], dh * 64, 64, p * 16, [[1, 16]]), [], [("og", h // 2), BK(7)])
                wo_group(l, a, n)
                P.fence()
                mlp(l, groups(H, T))
                P.fence()


        dry = Ring(None, None, None, None)
        emit_all(DryProg(), dry)
        P = Prog(nc, st)
        W = Ring(P, Wt, wst, dry.log)
        emit_all(P, W)
        assert W.nacq == len(dry.log)
        P.emit()
    return nc


def _pieces(conv_w1, conv_w2, mlp_w1, mlp_w2, w_kv, attn_wq, attn_wo):
    wst = np.empty((NPIECE, 128, SLOT), np.float32)

    def k8(Wsub):
        return Wsub.reshape(8, 128, 512).transpose(1, 0, 2).reshape(128, SLOT)

    def j4(Wsub):
        return Wsub.reshape(4, 128, 1024).transpose(1, 0, 2).reshape(128, SLOT)

    for l in range(2):
        for i in range(4):
            sub = np.concatenate([conv_w1[l][:, i * 256:(i + 1) * 256], conv_w1[l][:, 1024 + i * 256:1024 + (i + 1) * 256]], axis=1)
            wst[pid_w1(l, i)] = k8(sub)
        for i in range(2):
            wst[pid_w2(l, i)] = k8(conv_w2[l][:, i * 512:(i + 1) * 512])
    for l in range(4):
        for e in range(8):
            wst[pid_mlp(l, e, 0)] = k8(mlp_w1[l][:, e * 512:(e + 1) * 512])
            wst[pid_mlp(l, e, 1)] = j4(mlp_w2[l][e * 512:(e + 1) * 512, :])
    wst[PID_KV] = k8(w_kv)
    colperm = np.concatenate([np.arange(PERM[p] * 64, PERM[p] * 64 + 64) for p in range(16)])
    for l in (2, 3):
        wqp = attn_wq[l - 2][:, colperm]
        wop = attn_wo[l - 2]
        for i in range(2):
            wst[pid_wq(l, i)] = k8(wqp[:, i * 512:(i + 1) * 512])
            wst[pid_wo(l, i)] = k8(wop[:, i * 512:(i + 1) * 512])
    return wst


def _fm(v):
    return np.ascontiguousarray(v.reshape(8, 128).T)


_NC_CACHE = {}


def kernel(x_prompt, x_sample, state_conv, state_win_k, state_win_v,
           norm_mix_g, norm_mlp_g,
           conv_w1, conv_b1, conv_dw, conv_dwb, conv_ln_g, conv_ln_b, conv_w2, conv_b2,
           kv_norm_g, w_kv, k_norm_g,
           attn_wq, q_norm_g, attn_sinks, attn_wo,
           mlp_w1, mlp_w2):
    f = lambda a: np.ascontiguousarray(np.asarray(a, dtype=np.float32))
    x_prompt, x_sample, state_conv, state_win_k, state_win_v = map(f, (x_prompt, x_sample, state_conv, state_win_k, state_win_v))
    norm_mix_g, norm_mlp_g, conv_w1, conv_b1, conv_dw, conv_dwb = map(f, (norm_mix_g, norm_mlp_g, conv_w1, conv_b1, conv_dw, conv_dwb))
    conv_ln_g, conv_ln_b, conv_w2, conv_b2, kv_norm_g, w_kv, k_norm_g = map(f, (conv_ln_g, conv_ln_b, conv_w2, conv_b2, kv_norm_g, w_kv, k_norm_g))
    attn_wq, q_norm_g, attn_sinks, attn_wo, mlp_w1, mlp_w2 = map(f, (attn_wq, q_norm_g, attn_sinks, attn_wo, mlp_w1, mlp_w2))

    vecs = np.zeros((128, NV), np.float32)
    for l in range(4):
        vecs[:, VC["nmix"] + 8 * l:VC["nmix"] + 8 * l + 8] = _fm(norm_mix_g[l])
        vecs[:, VC["nmlp"] + 8 * l:VC["nmlp"] + 8 * l + 8] = _fm(norm_mlp_g[l])
    vecs[:, VC["kvn"]:VC["kvn"] + 8] = _fm(kv_norm_g)
    for l in range(2):
        vecs[:, VC["b1"] + 16 * l:VC["b1"] + 16 * l + 16] = conv_b1[l].reshape(16, 128).T
        vecs[:, VC["dwb"] + 8 * l:VC["dwb"] + 8 * l + 8] = _fm(conv_dwb[l])
        vecs[:, VC["lng"] + 8 * l:VC["lng"] + 8 * l + 8] = _fm(conv_ln_g[l])
        vecs[:, VC["lnb"] + 8 * l:VC["lnb"] + 8 * l + 8] = _fm(conv_ln_b[l])
        vecs[:, VC["b2"] + 8 * l:VC["b2"] + 8 * l + 8] = _fm(conv_b2[l])
        vecs[:, VC["dwT"] + 248 * l:VC["dwT"] + 248 * (l + 1)] = conv_dw[l].reshape(31, 8, 128).transpose(2, 0, 1).reshape(128, 248)
    vecs[:, VC["kng"]] = np.tile(k_norm_g, 2)
    for lb in range(2):
        vecs[:, VC["qng"] + lb] = np.tile(q_norm_g[lb], 2)
        vecs[:, VC["sink"] + 16 * lb:VC["sink"] + 16 * lb + 16] = np.broadcast_to(attn_sinks[lb][PERM], (128, 16))
    gtm = np.ascontiguousarray(np.broadcast_to(np.concatenate([k_norm_g, q_norm_g[0], q_norm_g[1]])[None, :], (NS, 192)))
    wst = _pieces(conv_w1, conv_w2, mlp_w1, mlp_w2, w_kv, attn_wq, attn_wo)

    ident = np.eye(128, dtype=np.float32)
    bd64 = np.zeros((128, 128), np.float32)
    bd64[:64, :64] = 1.0
    bd64[64:, 64:] = 1.0
    sel4 = np.zeros((120, 4), np.float32)
    for b in range(4):
        sel4[30 * b:30 * b + 30, b] = 1.0
    i16 = np.eye(16, dtype=np.float32)
    swp = np.zeros((128, 128), np.float32)
    for kk in range(128):
        swp[kk, (kk + 64) % 128] = 1.0
    s_ = np.arange(128)[:, None, None, None, None]
    kb_ = np.arange(2)[None, None, :, None, None]
    q_ = np.arange(128)[None, None, None, None, :]
    delta = 128 + q_ - (kb_ * 128 + s_)
    slope = np.array(SLOPES, np.float64).reshape(4, 4)[None, :, None, :, None]
    valid = (delta >= 0) & (delta <= 128)
    emask = np.where(valid, np.exp(-slope * delta), 0.0).astype(np.float32).reshape(128, 4096)
    ebs = np.zeros((128, 16), np.float32)
    for p in range(16):
        ebs[:, p] = -SLOPES[PERM[p]] * (128 - np.arange(128))

    in_maps = []
    for core in range(8):
        b, r = core // 4, core % 4
        xin = np.zeros((TP, D), np.float32)
        if r == 0:
            xin[H:] = x_prompt[b, 0:NO]
        else:
            xin[:] = x_prompt[b, r * NO - H:(r + 1) * NO]
        sl = slice(core * NS, (core + 1) * NS)
        hm = np.zeros((128, 2), np.float32)
        hm[:, 0] = 0.0 if r == 0 else 1.0
        in_maps.append({
            "xin": xin, "xs": np.ascontiguousarray(x_sample[sl, 0, :]),
            "sconv": np.ascontiguousarray(state_conv[:, sl]),
            "swk": np.ascontiguousarray(state_win_k[sl].reshape(NS, 128, 256)),
            "swv": np.ascontiguousarray(state_win_v[sl].reshape(NS, 128, 256)),
            "hm": hm, "vecs": vecs, "gtm": gtm, "cdw": conv_dw, "wst": wst,
            "ident": ident, "bd64": bd64, "sel4": sel4, "i16": i16, "emask": emask, "ebs": ebs, "swp": swp,
        })
    if "nc" not in _NC_CACHE:
        _NC_CACHE["nc"] = build_program()
    nc = _NC_CACHE["nc"]
    res = run_bass_kernel_spmd(nc, in_maps, core_ids=list(range(8)))
    rs = res.results

    y_prompt = np.empty((2, 8192, D), np.float32)
    y_sample = np.empty((128, 1, D), np.float32)
    ncp = np.empty((2, 2, 30, D), np.float32)
    ncs = np.empty((2, 128, 30, D), np.float32)
    nkp = np.empty((2, 128, 4, 64), np.float32)
    nvp = np.empty((2, 128, 4, 64), np.float32)
    nks = np.empty((128, 128, 4, 64), np.float32)
    nvs = np.empty((128, 128, 4, 64), np.float32)
    for core in range(8):
        b, r = core // 4, core % 4
        o = rs[core]
        sl = slice(core * NS, (core + 1) * NS)
        y_prompt[b, r * NO:(r + 1) * NO] = o["y"][:NO]
        y_sample[sl, 0] = o["y"][NO:]
        ncs[:, sl] = o["ncs"]
        nks[sl] = o["nks"].reshape(NS, 128, 4, 64)
        nvs[sl] = o["nvs"].reshape(NS, 128, 4, 64)
        if r == 3:
            ncp[:, b] = o["ncp"]
            nkp[b] = o["nkp"].reshape(128, 4, 64)
            nvp[b] = o["nvp"].reshape(128, 4, 64)
    return (y_prompt, y_sample, ncp, ncs, nkp, nvp, nks, nvs)
```
